# Optimizing a Trainium2 kernel written in Bass

```python
import math
import jax, jax.numpy as jnp
from jax import lax
import numpy as np


D_MODEL = 1024
BATCH = 8
SEQ = 4096
DEPTH = 4

CHUNK = 64
N_META = 16
Q_BLOCK = 128
D_MIX = D_MODEL
NORM_EPS = 1e-6
SUBLN_EPS = 1e-5
NEG = -1e30

FOX_HEADS = 4
FOX_HD = D_MODEL // 16
FOX_W = FOX_HEADS * FOX_HD
FORGET_BIAS_CENTER = 3.0

DIFF_HEADS = 4
DIFF_HD = D_MODEL // 16
DIFF_VD = 2 * DIFF_HD
DIFF_W = DIFF_HEADS * DIFF_VD
ALIBI_SLOPES = tuple(2.0 ** (-8.0 * (h + 1) / DIFF_HEADS) for h in range(DIFF_HEADS))

POOL_WINDOWS = (2, 4, 8, 16)
POOL_GROUPS = len(POOL_WINDOWS)
POOL_W = D_MIX - FOX_W - DIFF_W
POOL_GD = POOL_W // POOL_GROUPS

IN_SIZES = (FOX_W, FOX_W, FOX_W, FOX_W, FOX_HEADS,
            2 * DIFF_HEADS * DIFF_HD, 2 * DIFF_HEADS * DIFF_HD, DIFF_W, DIFF_W,
            POOL_W, POOL_W)
IN_DIM = sum(IN_SIZES)
IN_OFFSETS = tuple(int(o) for o in np.cumsum(IN_SIZES)[:-1])

kernel_name = 'hymba_fox_diff_pool_chunk_causal'


def rmsnorm(x, g, eps=NORM_EPS):
    xf = x.astype(jnp.float32)
    y = xf * lax.rsqrt(jnp.mean(xf * xf, axis=-1, keepdims=True) + eps) * g.astype(jnp.float32)
    return y.astype(x.dtype)


def chunk_ids(pos):
    return jnp.where(pos < N_META, 0, 1 + (pos - N_META) // CHUNK)


def chunk_end(p):
    return N_META + CHUNK * ((p - N_META) // CHUNK + 1)


def forgetting_attention(q, k, v, c):
    Lp = q.shape[1]
    pos = jnp.arange(Lp)
    scale = FOX_HD ** -0.5
    outs = []
    for q0 in range(0, Lp, Q_BLOCK):
        q1 = q0 + Q_BLOCK
        kend = q1
        s = jnp.einsum('bqhd,bkhd->bhqk', q[:, q0:q1], k[:, :kend]).astype(jnp.float32) * scale
        s = s + c[:, :, q0:q1, None] - c[:, :, None, :kend]
        mask = pos[None, :kend] <= pos[q0:q1, None]
        p = jax.nn.softmax(jnp.where(mask, s, NEG), axis=-1).astype(v.dtype)
        outs.append(jnp.einsum('bhqk,bkhd->bqhd', p, v[:, :kend]))
    return jnp.concatenate(outs, axis=1)


def differential_attention(q, k, v, lam):
    Lp = q.shape[1]
    pos = jnp.arange(Lp)
    chunk = chunk_ids(pos)
    slopes = jnp.asarray(ALIBI_SLOPES, jnp.float32)
    scale = DIFF_HD ** -0.5
    outs = []
    for q0 in range(0, Lp, Q_BLOCK):
        q1 = q0 + Q_BLOCK
        kend = min(Lp, chunk_end(q1 - 1))
        s = jnp.einsum('bqhcd,bkhcd->bchqk', q[:, q0:q1], k[:, :kend]).astype(jnp.float32) * scale
        dist = jnp.abs(pos[q0:q1, None] - pos[None, :kend]).astype(jnp.float32)
        s = s - slopes[:, None, None] * dist
        mask = chunk[None, :kend] <= chunk[q0:q1, None]
        p = jax.nn.softmax(jnp.where(mask, s, NEG), axis=-1)
        w = (p[:, 0] - lam * p[:, 1]).astype(v.dtype)
        outs.append(jnp.einsum('bhqk,bkhe->bqhe', w, v[:, :kend]))
    return jnp.concatenate(outs, axis=1)


def pooling_mixer(u, w_pool, pool_scale):
    B, Lp, _ = u.shape
    uf = u.astype(jnp.float32).reshape(B, Lp, POOL_GROUPS, POOL_GD)
    cs = jnp.cumsum(uf, axis=1)
    cs0 = jnp.pad(cs, ((0, 0), (1, 0), (0, 0), (0, 0)))
    t = jnp.arange(Lp)
    win = jnp.asarray(POOL_WINDOWS)
    lo = jnp.maximum(t[:, None] + 1 - win[None, :], 0)
    lower = cs0[:, lo, jnp.arange(POOL_GROUPS)[None, :]]
    cnt = (t[:, None] + 1 - lo).astype(jnp.float32)
    pooled = (cs - lower) / cnt[None, :, :, None] - uf
    y = jnp.einsum('blgc,gcd->blgd', pooled.astype(u.dtype), w_pool)
    return y.reshape(B, Lp, POOL_W) * pool_scale


def setup_inputs(seed: int = 0) -> dict:
    key = jax.random.key(seed)
    ks = jax.random.split(key, 14)
    f32 = jnp.float32
    x = jax.random.normal(ks[0], (BATCH, SEQ, D_MODEL), f32)
    meta_tokens = jax.random.normal(ks[1], (N_META, D_MODEL), f32)
    norm_g = 1.0 + 0.02 * jax.random.normal(ks[2], (DEPTH, D_MODEL), f32)
    w_in = jax.random.normal(ks[3], (DEPTH, D_MODEL, IN_DIM), f32) * D_MODEL ** -0.5
    b_f = FORGET_BIAS_CENTER + 0.5 * jax.random.normal(ks[4], (DEPTH, FOX_HEADS), f32)
    lam_q1 = 0.1 * jax.random.normal(ks[5], (DEPTH, DIFF_HD), f32)
    lam_k1 = 0.1 * jax.random.normal(ks[6], (DEPTH, DIFF_HD), f32)
    lam_q2 = 0.1 * jax.random.normal(ks[7], (DEPTH, DIFF_HD), f32)
    lam_k2 = 0.1 * jax.random.normal(ks[8], (DEPTH, DIFF_HD), f32)
    subln_g = 1.0 + 0.02 * jax.random.normal(ks[9], (DEPTH, DIFF_VD), f32)
    w_pool = jax.random.normal(ks[10], (DEPTH, POOL_GROUPS, POOL_GD, POOL_GD), f32) * POOL_GD ** -0.5
    pool_scale = 1.0 + 0.1 * jax.random.normal(ks[11], (DEPTH, POOL_W), f32)
    w_out = jax.random.normal(ks[12], (DEPTH, D_MIX, D_MODEL), f32) * D_MIX ** -0.5
    final_g = 1.0 + 0.02 * jax.random.normal(ks[13], (D_MODEL,), f32)
    return {'x': x, 'meta_tokens': meta_tokens, 'norm_g': norm_g, 'w_in': w_in, 'b_f': b_f,
            'lam_q1': lam_q1, 'lam_k1': lam_k1, 'lam_q2': lam_q2, 'lam_k2': lam_k2,
            'subln_g': subln_g, 'w_pool': w_pool, 'pool_scale': pool_scale,
            'w_out': w_out, 'final_g': final_g}


def reference(x, meta_tokens, norm_g, w_in, b_f, lam_q1, lam_k1, lam_q2, lam_k2,
              subln_g, w_pool, pool_scale, w_out, final_g):
    B, S, D = x.shape
    L = N_META + S
    Lp = -(-L // Q_BLOCK) * Q_BLOCK
    meta = jnp.broadcast_to(meta_tokens.astype(x.dtype)[None], (B, N_META, D))
    h = jnp.concatenate([meta, x], axis=1)
    h = jnp.pad(h, ((0, 0), (0, Lp - L), (0, 0)))
    for l in range(DEPTH):
        hn = rmsnorm(h, norm_g[l])
        proj = jnp.einsum('bld,de->ble', hn, w_in[l])
        fq, fk, fv, fz, fg, dq, dk, dv, dz, pu, pz = jnp.split(proj, IN_OFFSETS, axis=-1)

        log_f = jax.nn.log_sigmoid(fg.astype(jnp.float32) + b_f[l].astype(jnp.float32))
        c = jnp.transpose(jnp.cumsum(log_f, axis=1), (0, 2, 1))
        a_out = forgetting_attention(fq.reshape(B, Lp, FOX_HEADS, FOX_HD),
                                     fk.reshape(B, Lp, FOX_HEADS, FOX_HD),
                                     fv.reshape(B, Lp, FOX_HEADS, FOX_HD), c)
        a_out = a_out.reshape(B, Lp, FOX_W) * jax.nn.silu(fz)

        lambda_init = 0.8 - 0.6 * math.exp(-0.3 * l)
        lam = (jnp.exp(jnp.sum(lam_q1[l].astype(jnp.float32) * lam_k1[l].astype(jnp.float32)))
               - jnp.exp(jnp.sum(lam_q2[l].astype(jnp.float32) * lam_k2[l].astype(jnp.float32)))
               + lambda_init)
        b_out = differential_attention(dq.reshape(B, Lp, DIFF_HEADS, 2, DIFF_HD),
                                       dk.reshape(B, Lp, DIFF_HEADS, 2, DIFF_HD),
                                       dv.reshape(B, Lp, DIFF_HEADS, DIFF_VD), lam)
        b_out = rmsnorm(b_out, subln_g[l], SUBLN_EPS) * (1.0 - lambda_init)
        b_out = b_out.reshape(B, Lp, DIFF_W) * jax.nn.silu(dz)

        c_out = pooling_mixer(pu, w_pool[l], pool_scale[l]) * jax.nn.silu(pz)

        mix = jnp.concatenate([a_out, b_out, c_out], axis=-1)
        h = h + jnp.einsum('ble,ed->bld', mix, w_out[l])
    y = rmsnorm(h[:, N_META:N_META + S], final_g)
    return y
```

```python
import contextlib
import numpy as np
import ml_dtypes
import concourse.bass as bass
import concourse.mybir as mybir
from concourse.bass_utils import run_bass_kernel_spmd

F32 = mybir.dt.float32
BF16 = mybir.dt.bfloat16
AF = mybir.ActivationFunctionType
ALU = mybir.AluOpType
AX = mybir.AxisListType

L = 4
D = 1024
S_LEN = 4096
T = 4112
NB = 33
SLOPES = [2.0 ** (-8.0 * (h + 1) / 4) for h in range(4)]
NEGM = -30000.0
COLG = [(0, 16)] + [(16 + 512 * i, 512) for i in range(8)]
OFF_FOX = [256 * h for h in range(4)]
OFF_DIFF = [1024 + 512 * h for h in range(4)]
OFF_POOL = 3072
OFF_FG = 3584
GQ_DIFF = [256, 512, 512, 512]


def blk(b):
    return (0, 16) if b == 0 else (16 + 128 * (b - 1), 128)


class Op:
    __slots__ = ("eng", "fn", "deps", "inc", "val", "dma", "layer")


class DSem:
    __slots__ = ("h", "total", "last")


ENGS = ("pe", "act", "dve", "pool", "sp")


class Prog:
    def __init__(self, nc, es, nlayers):
        self.nc = nc
        self.es = es
        self.ops = {e: [] for e in ENGS}
        self.res = {}
        self.layer = 0
        self.esem = {}
        for e in ("pe", "act", "dve", "pool", "sp"):
            for l in range(nlayers):
                self.esem[(e, l)] = es.enter_context(nc.semaphore(f"s_{e}_{l}"))
        self.dsem = {}
        self.pending = {}

    def barrier(self):
        keys = list(self.res.keys())
        o = None
        for e in ENGS:
            o = self.add(e, lambda en: en.nop(), w=keys, force=True)
        self.pending = {e: o for e in ENGS}

    def _ds(self, name):
        d = self.dsem.get(name)
        if d is None:
            d = DSem()
            d.h = self.es.enter_context(self.nc.semaphore("d_" + name))
            d.total = 0
            d.last = None
            self.dsem[name] = d
        return d

    def add(self, eng, fn, r=(), w=(), dma=None, force=False):
        o = Op()
        o.eng = eng
        o.fn = fn
        o.dma = dma
        o.layer = self.layer
        o.inc = False
        o.val = None
        deps = []
        w = list(w) + [k for k in r if isinstance(k, tuple) and k[0] == "ps" and k not in w]
        for k in r:
            st = self.res.get(k)
            if st is not None and st[0] is not None:
                deps.append(st[0])
        for k in w:
            st = self.res.get(k)
            if st is not None:
                if st[0] is not None:
                    deps.append(st[0])
                deps.extend(st[1].values())
                deps.extend(st[2])
        if dma is not None:
            ds = self._ds(dma[0])
            if ds.last is not None:
                deps.append(ds.last)
            ds.last = o
        pb = self.pending.pop(eng, None)
        if pb is not None:
            deps.append(pb)
        o.deps = []
        seen = set()
        for d in deps:
            if d is o or id(d) in seen:
                continue
            seen.add(id(d))
            if d.dma is None and o.dma is None and d.eng == eng and eng == "pe" and not force:
                continue
            o.deps.append(d)
            d.inc = True
        for k in r:
            st = self.res.get(k)
            if st is None:
                st = [None, {}, []]
                self.res[k] = st
            if dma is not None:
                st[2].append(o)
            else:
                st[1][eng] = o
        for k in w:
            self.res[k] = [o, {}, []]
        self.ops[eng].append(o)
        return o

    def finalize(self):
        cnt = {}
        for e in ENGS:
            for o in self.ops[e]:
                if o.dma is not None:
                    ds = self.dsem[o.dma[0]]
                    ds.total += 16 * o.dma[1]
                    o.val = (ds.h, ds.total, "d_" + o.dma[0])
                elif o.inc:
                    k = (e, o.layer)
                    cnt[k] = cnt.get(k, 0) + 1
                    o.val = (self.esem[k], cnt[k], f"{e}{o.layer}")

    def emit(self, eng, e):
        waited = {}
        for o in self.ops[eng]:
            for d in o.deps:
                sem, val, name = d.val
                if waited.get(name, 0) >= val:
                    continue
                e.wait_ge(sem, val)
                waited[name] = val
            ins = o.fn(e)
            if o.dma is not None:
                sem = o.val[0]
                assert len(ins) == o.dma[1], (len(ins), o.dma)
                for i in ins:
                    i.then_inc(sem, 16)
            elif o.inc:
                ins.then_inc(o.val[0], 1)


def build_program(nlayers=L, dbg=False, units=None):
    nc = bass.Bass("TRN2", target_bir_lowering=False)
    es = contextlib.ExitStack()
    with es:
        kin = "ExternalInput"
        x_d = nc.dram_tensor("x", [S_LEN, D], F32, kind=kin).ap()
        meta_d = nc.dram_tensor("meta", [16, D], F32, kind=kin).ap()
        normg_d = nc.dram_tensor("norm_g", [L, D], F32, kind=kin).ap()
        win_d = nc.dram_tensor("w_in", [L, 128, 8, 3588], F32, kind=kin).ap()
        bf_d = nc.dram_tensor("b_f", [L, 4, 1], F32, kind=kin).ap()
        lam_d = nc.dram_tensor("lam", [L, 256], F32, kind=kin).ap()
        subg_d = nc.dram_tensor("subln_g", [L, 128], F32, kind=kin).ap()
        wpool_d = nc.dram_tensor("w_pool", [L, 4, 64, 64], F32, kind=kin).ap()
        pscale_d = nc.dram_tensor("pool_scale", [L, 128, 2], F32, kind=kin).ap()
        wout_d = nc.dram_tensor("w_out", [L, 128, 8, D], F32, kind=kin).ap()
        fing_d = nc.dram_tensor("final_g", [1, D], F32, kind=kin).ap()
        maskc_d = nc.dram_tensor("c_maskc", [128, 128], F32, kind=kin).ap()
        mdiff_d = nc.dram_tensor("c_mdiff", [128, 4, 128], F32, kind=kin).ap()
        atab_d = nc.dram_tensor("c_atab", [128, 4, 35], F32, kind=kin).ap()
        ameta_d = nc.dram_tensor("c_ameta", [128, 4, 17], F32, kind=kin).ap()
        onehot_d = nc.dram_tensor("c_onehot", [4, 4, 128], F32, kind=kin).ap()
        invc_d = nc.dram_tensor("c_invc", [128, 2, 16], F32, kind=kin).ap()
        idsw_d = nc.dram_tensor("c_idsw", [128, 128], F32, kind=kin).ap()
        mdsw_d = nc.dram_tensor("c_mdiffsw", [128, 4, 128], F32, kind=kin).ap()
        y_d = nc.dram_tensor("y", [S_LEN, D], F32, kind="ExternalOutput").ap()
        skind = "ExternalOutput" if dbg else "Internal"
        h_d = nc.dram_tensor("h_scr", [T, D], F32, kind=skind).ap()
        mixT_d = nc.dram_tensor("mixT_scr", [8, 128, T], BF16, kind=skind).ap()

        uniq = [0]

        def sb(name, shape, dt, st=es):
            uniq[0] += 1
            return st.enter_context(nc.sbuf_tensor(f"{name}_{uniq[0]}", shape, dt))

        P = Prog(nc, es, nlayers)
        peak = [0]

        def check_sbuf(tag):
            return
            try:
                with nc.sbuf_tensor(f"probe_{tag}_{uniq[0]}", [128, 400000], F32):
                    pass
            except AssertionError as ex:
                msg = str(ex)
                have = int(msg.split("have ")[1].split("B")[0])
                top = int(msg.split("top=")[1].split(")")[0])
                used = top - have
                peak[0] = max(peak[0], used)
                assert used <= 192 * 1024 - 256, f"SBUF over budget at {tag}: {used}"

        psS = [es.enter_context(nc.psum_tensor(f"psS{i}", [128, 2, 512], F32)) for i in range(2)]
        psA = es.enter_context(nc.psum_tensor("psA", [128, 3, 512], F32))
        psT = es.enter_context(nc.psum_tensor("psT", [128, 1024], BF16))

        def sbank(i):
            return psS[i // 2][:, i % 2, :], ("ps", i)

        KA = [("ps", 4), ("ps", 5), ("ps", 6)]
        KT_ = ("ps", 7)
        rot = [0]

        def next_sbank():
            i = rot[0] % 4
            rot[0] += 1
            return sbank(i)

        hnT = sb("hnT", [128, 8, T], BF16)
        ident = sb("ident", [128, 128], BF16)
        identf = sb("identf", [128, 128], F32)
        maskc = sb("maskc", [128, 128], F32)
        mdiff = sb("mdiff", [128, 4, 128], F32)
        atab = sb("atab", [128, 4, 35], F32)
        ameta = sb("ameta", [128, 4, 17], F32)
        onehot = sb("onehot", [4, 4, 128], F32)
        invc = sb("invc", [128, 2, 16], F32)
        cneg = sb("cneg", [128, 2], F32)
        g_bc = sb("g_bc", [128, D], F32)
        lam_bc = sb("lam_bc", [128, 256], F32)
        lamv = sb("lamv", [128, 8], F32)
        subg = sb("subg", [128, 128], F32)
        subg2 = sb("subg2", [128, 128], F32)
        bfcol = sb("bfcol", [4, 2], F32)
        pscale = sb("pscale", [128, 2], F32)
        pscale2 = sb("pscale2", [128, 2], F32)
        wp_stage = sb("wp_stage", [128, 2, 128], F32)
        bd = sb("bd", [128, 2, 128], BF16)
        lay = {}
        junk64 = sb("junk64", [128, 64], F32)
        maskc_bf = sb("maskc_bf", [128, 128], BF16)
        mdiff_bf = sb("mdiff_bf", [128, 4, 128], BF16)
        mdsw_bf = sb("mdsw_bf", [128, 4, 128], BF16)
        idsw_bf = sb("idsw_bf", [128, 128], BF16)

        def f_const(e):
            return [
                e.dma_start(out=maskc[:], in_=maskc_d),
                e.dma_start(out=mdiff[:], in_=mdiff_d),
                e.dma_start(out=atab[:], in_=atab_d),
                e.dma_start(out=ameta[:], in_=ameta_d),
                e.dma_start(out=onehot[:], in_=onehot_d),
                e.dma_start(out=invc[:], in_=invc_d),
            ]

        P.add("sp", f_const, w=["consts"], dma=("const", 6))
        P.add("pool", lambda e: e.memset(ident[:], 0.0), w=["ident"])
        P.add("pool", lambda e: e.affine_select(out=ident[:], in_=ident[:], pattern=[[-1, 128]],
                                                 compare_op=ALU.not_equal, fill=1.0, base=0,
                                                 channel_multiplier=1), r=["ident"], w=["ident"])
        P.add("pool", lambda e: e.memset(identf[:], 0.0), w=["identf"])
        P.add("pool", lambda e: e.affine_select(out=identf[:], in_=identf[:], pattern=[[-1, 128]],
                                                 compare_op=ALU.not_equal, fill=1.0, base=0,
                                                 channel_multiplier=1), r=["identf"], w=["identf"])
        P.add("pool", lambda e: e.memset(cneg[:, 0:1], -0.5), w=["cneg"])
        P.add("pool", lambda e: e.tensor_copy(out=maskc_bf[:], in_=maskc[:]), r=["consts"], w=["mask_bf"])
        P.add("pool", lambda e: e.tensor_copy(out=mdiff_bf[:], in_=mdiff[:]), r=["consts"], w=["mask_bf"])
        P.add("sp", lambda e: [e.dma_start(out=mdiff[:], in_=mdsw_d)], r=["mask_bf"], w=["consts2"], dma=("const2", 1))
        P.add("pool", lambda e: e.tensor_copy(out=mdsw_bf[:], in_=mdiff[:]), r=["consts2"], w=["mask_bf2"])
        P.add("sp", lambda e: [e.dma_start(out=maskc[:], in_=idsw_d)], r=["mask_bf"], w=["consts3"], dma=("const3", 1))
        P.add("pool", lambda e: e.tensor_copy(out=idsw_bf[:], in_=maskc[:]), r=["consts3"], w=["mask_bf2"])

        def norm_tile(b, h_sb, hkey, nb_):
            c0, n = blk(b)
            pb_ = b % 2
            jk, ss, rs, hb = nb_["junk"][pb_], nb_["ss"][pb_], nb_["rs"][pb_], nb_["hb"][b % 2]
            hbk = ("hb", b % 2)
            P.add("act", lambda e: e.activation(out=jk[0:n, :], in_=h_sb[0:n, :], func=AF.Square,
                                                 accum_out=ss[0:n, 0:1]), r=[hkey], w=[("nt_junk", pb_), ("nt_ss", pb_)])
            P.add("dve", lambda e: e.tensor_scalar(out=ss[0:n, 1:2], in0=ss[0:n, 0:1], scalar1=1.0 / D,
                                                    scalar2=1e-6, op0=ALU.mult, op1=ALU.add),
                  r=[("nt_ss", pb_)], w=[("nt_ss2", pb_)])
            P.add("pool", lambda e: e.tensor_tensor(out=rs[0:n, 0:1], in0=ss[0:n, 1:2], in1=cneg[0:n, 0:1],
                                                     op=ALU.pow), r=[("nt_ss2", pb_), "cneg"], w=[("nt_rs", pb_)])
            P.add("dve", lambda e: e.scalar_tensor_tensor(out=hb[0:n, :], in0=h_sb[0:n, :], scalar=rs[0:n, 0:1],
                                                           in1=g_bc[0:n, :], op0=ALU.mult, op1=ALU.mult),
                  r=[hkey, ("nt_rs", pb_), "g_bc"], w=[hbk])
            for c in range(8):
                P.add("pe", lambda e, c=c: e.transpose(psT[:, c * 128:c * 128 + n], hb[0:n, c * 128:(c + 1) * 128],
                                                        ident[0:n, 0:n]),
                      r=[hbk, "ident"], w=[KT_])
            P.add("dve", lambda e: e.tensor_copy(out=hnT[:, :, c0:c0 + n],
                                                  in_=psT[:, :].rearrange("p (c t) -> p c t", c=8)[:, :, 0:n]),
                  r=[KT_], w=[("hnT", b)])

        def hnT_keys(c0, n):
            ks = []
            for b in range(NB):
                bc, bn = blk(b)
                if bc < c0 + n and c0 < bc + bn:
                    ks.append(("hnT", b))
            return ks

        ALL_HNT = [("hnT", b) for b in range(NB)]

        def load_g(src_ap):
            P.add("sp", lambda e: [e.dma_start(out=g_bc[:], in_=src_ap.partition_broadcast(128))],
                  w=["g_bc"], dma=("gbc", 1))

        def phase_a():
            with contextlib.ExitStack() as st:
                nb_ = dict(junk=[sb(f"a_junk{i}", [128, D], BF16, st) for i in range(2)],
                           ss=[sb(f"a_ss{i}", [128, 2], F32, st) for i in range(2)],
                           rs=[sb(f"a_rs{i}", [128, 1], F32, st) for i in range(2)],
                           hb=[sb(f"a_hb{i}", [128, D], BF16, st) for i in range(2)])
                hs = [sb(f"a_h{i}", [128, D], F32, st) for i in range(6)]
                load_g(normg_d[0:1, :])
                if units is None:
                    prefetch_w(0, OFF_FOX[0], 256)
                for b in range(NB):
                    c0, n = blk(b)
                    slot = b % 6
                    src = meta_d if b == 0 else x_d[(b - 1) * 128:b * 128, :]
                    P.add("sp", lambda e, slot=slot, n=n, src=src: [e.dma_start(out=hs[slot][0:n, :], in_=src)],
                          w=[("ah", slot)], dma=(f"ah{slot}", 1))
                    norm_tile(b, hs[slot], ("ah", slot), nb_)
                P.barrier()

        wslots = [sb(f"wslot{i}", [128, 8, 512], BF16) for i in range(2)]
        accS = sb("accS", [128, 4, 2, 130], F32)
        wfg = sb("wfg", [128, 8, 4], BF16)
        wfg32 = sb("wfg32", [128, 8, 4], F32)
        wctr = [0]

        wstage = [sb(f"wstage{i}", [128, 1024], F32) for i in range(2)]
        stctr = [0]

        prefetched = {}

        def load_w_steps(l, off, ncols):
            s = wctr[0] % 2
            wctr[0] += 1
            cpc = 1024 // ncols
            pieces = []
            for c0 in range(0, 8, cpc):
                ss = stctr[0] % 2
                stctr[0] += 1
                stv = wstage[ss][:, 0:cpc * ncols].rearrange("p (c n) -> p c n", c=cpc)
                pieces.append((c0, ss, stv))

            def dma(p):
                c0, ss, stv = p
                P.add("sp", lambda e: [e.dma_start(out=stv, in_=win_d[l, :, c0:c0 + cpc, off:off + ncols])],
                      w=[("wst", ss)], dma=(f"wst{ss}", 1))

            def cast(p):
                c0, ss, stv = p
                P.add("act", lambda e: e.activation(out=wslots[s][:, c0:c0 + cpc, 0:ncols], in_=stv, func=AF.Copy),
                      r=[("wst", ss)], w=[("w", s)])

            def stA():
                for p in pieces[0:2]:
                    dma(p)

            def stB():
                for p in pieces[0:2]:
                    cast(p)
                for p in pieces[2:4]:
                    dma(p)

            def stC():
                for p in pieces[2:4]:
                    cast(p)
            return wslots[s], ("w", s), [stA, stB, stC]

        def prefetch_w(l, off, ncols):
            W_, wk_, steps_ = load_w_steps(l, off, ncols)
            for st_ in steps_:
                st_()
            prefetched[(l, off)] = (W_, wk_, [])

        def load_w(l, off, ncols):
            key = (l, off)
            if key in prefetched:
                W, wk, steps = prefetched.pop(key)
            else:
                W, wk, steps = load_w_steps(l, off, ncols)
            for st_ in steps:
                st_()
            return W, wk

        def layer_prep(l):
            lam_init = 0.8 - 0.6 * float(np.exp(-0.3 * l))

            def f(e):
                return [
                    e.dma_start(out=lam_bc[:], in_=lam_d[l:l + 1, :].partition_broadcast(128)),
                    e.dma_start(out=subg[:], in_=subg_d[l:l + 1, :].partition_broadcast(128)),
                    e.dma_start(out=bfcol[0:4, 0:1], in_=bf_d[l]),
                    e.dma_start(out=pscale[:], in_=pscale_d[l]),
                ]
            P.add("sp", f, w=["lprep"], dma=("lprep", 4))
            P.add("pool", lambda e: e.memset(wp_stage[:], 0.0), w=["wp_stage"])

            def f2(e):
                return [
                    e.dma_start(out=wp_stage[0:64, 0, 0:64], in_=wpool_d[l, 0]),
                    e.dma_start(out=wp_stage[64:128, 0, 64:128], in_=wpool_d[l, 1]),
                    e.dma_start(out=wp_stage[0:64, 1, 0:64], in_=wpool_d[l, 2]),
                    e.dma_start(out=wp_stage[64:128, 1, 64:128], in_=wpool_d[l, 3]),
                ]
            P.add("sp", f2, w=["wp_stage"], dma=("wpst", 4))
            P.add("pool", lambda e: e.tensor_copy(out=bd[:], in_=wp_stage[:]), r=["wp_stage"], w=["bd"])
            for i in range(2):
                P.add("dve", lambda e, i=i: e.tensor_tensor(out=junk64[:], in0=lam_bc[:, 128 * i:128 * i + 64],
                                                             in1=lam_bc[:, 128 * i + 64:128 * i + 128], op=ALU.mult),
                      r=["lprep"], w=["junk64"])
                P.add("dve", lambda e, i=i: e.tensor_reduce(out=lamv[:, i:i + 1], in_=junk64[:], axis=AX.X, op=ALU.add),
                      r=["junk64"], w=[("lamv", i)])
            P.add("act", lambda e: e.activation(out=lamv[:, 2:4], in_=lamv[:, 0:2], func=AF.Exp),
                  r=[("lamv", 0), ("lamv", 1)], w=["lamv_e"])
            P.add("dve", lambda e: e.tensor_tensor(out=lamv[:, 4:5], in0=lamv[:, 2:3], in1=lamv[:, 3:4], op=ALU.subtract),
                  r=["lamv_e"], w=["lamv_d"])
            P.add("dve", lambda e: e.tensor_scalar(out=lamv[:, 5:6], in0=lamv[:, 4:5], scalar1=lam_init, scalar2=-1.0,
                                                    op0=ALU.add, op1=ALU.mult), r=["lamv_d"], w=["lamneg"])
            P.add("dve", lambda e: e.tensor_scalar(out=subg2[:], in0=subg[:], scalar1=(1.0 - lam_init) * 0.5, scalar2=None,
                                                    op0=ALU.mult), r=["lprep"], w=["subg2"])
            P.add("dve", lambda e: e.tensor_scalar(out=pscale2[:], in0=pscale[:], scalar1=0.5, scalar2=None, op0=ALU.mult),
                  r=["lprep"], w=["pscale2"])
            P.add("dve", lambda e: e.tensor_scalar(out=bfcol[0:4, 1:2], in0=bfcol[0:4, 0:1], scalar1=-1.0, scalar2=None,
                                                    op0=ALU.mult), r=["lprep"], w=["bfneg"])

        def fox_prep(l):
            biasF, cdiff = lay["biasF"], lay["cdiff"]
            KTl = lay["KT"]
            P.add("pool", lambda e: e.memset(KTl[64:128, :], 0.0), w=["KT", "KTone", "cdiff"])
            with contextlib.ExitStack() as st:
                sp_t = sb("fx_sp", [4, T], F32, st)
                cpos = sb("fx_cpos", [4, T], F32, st)
                ones4 = sb("fx_ones", [4, 512], F32, st)
                cK = sb("fx_cK", [128, 33, 4], F32, st)
                cRb = sb("fx_cRb", [128, 4, 9], F32, st)
                check_sbuf("foxprep")
                P.add("sp", lambda e: [e.dma_start(out=wfg32[:], in_=win_d[l, :, :, OFF_FG:OFF_FG + 4])],
                      w=["wfg32"], dma=("wfg", 1))
                P.add("pool", lambda e: e.tensor_copy(out=wfg[:], in_=wfg32[:]), r=["wfg32"], w=["wfg"])
                P.add("pool", lambda e: e.memset(ones4[:], 1.0), w=["fx_ones"])
                for (c0, n) in COLG:
                    ps, pk = next_sbank()
                    for c in range(8):
                        P.add("pe", lambda e, c=c, ps=ps, c0=c0, n=n: e.matmul(ps[0:4, 0:n], wfg[:, c, 0:4], hnT[:, c, c0:c0 + n],
                                                                                start=(c == 0), stop=(c == 7)),
                              r=["wfg"] + hnT_keys(c0, n), w=[pk])
                    P.add("act", lambda e, ps=ps, c0=c0, n=n: e.activation(out=sp_t[0:4, c0:c0 + n], in_=ps[0:4, 0:n], func=AF.Exp,
                                                                            scale=-1.0, bias=bfcol[0:4, 1:2]),
                          r=[pk, "bfneg"], w=[("fx_sp", c0)])
                    P.add("act", lambda e, c0=c0, n=n: e.activation(out=sp_t[0:4, c0:c0 + n], in_=sp_t[0:4, c0:c0 + n], func=AF.Ln,
                                                                     bias=1.0),
                          r=[("fx_sp", c0)], w=[("fx_sp", c0)])
                for gi_, (c0, n) in enumerate(COLG):
                    init = 0.0 if gi_ == 0 else cpos[0:4, c0 - 1:c0]
                    P.add("dve", lambda e, c0=c0, n=n, init=init: e.tensor_tensor_scan(out=cpos[0:4, c0:c0 + n], data0=ones4[0:4, 0:n],
                                                                                      data1=sp_t[0:4, c0:c0 + n], initial=init,
                                                                                      op0=ALU.mult, op1=ALU.add),
                          r=[("fx_sp", c0), "fx_ones", "fx_cpos"], w=["fx_cpos"])
                psAf = psA[:, 0, :]
                for b in range(NB):
                    c0, n = blk(b)
                    P.add("pe", lambda e, b=b, c0=c0, n=n: e.transpose(psAf[0:n, b * 4:b * 4 + 4], cpos[0:4, c0:c0 + n],
                                                                        identf[0:4, 0:4]),
                          r=["fx_cpos", "identf"], w=[KA[0]])
                P.add("pool", lambda e: e.memset(cK[:].rearrange("p b h -> p (b h)"), 0.0), w=["fx_cK", "fx_cK0"])
                P.add("dve", lambda e: e.tensor_copy(out=cK[0:16, 0, :], in_=psAf[0:16, 0:4]),
                      r=[KA[0]], w=["fx_cK0"])
                P.add("dve", lambda e: e.tensor_copy(out=cK[:, 1:33, :].rearrange("p b h -> p (b h)"), in_=psAf[:, 4:132]),
                      r=[KA[0]], w=["fx_cK"])
                psB = psA[:, 1, :]
                refcols = [0] + [16 + 512 * gi + 256 for gi in range(8)]
                for h in range(4):
                    P.add("pe", lambda e, h=h: e.matmul(psB[:, h * 9:h * 9 + 1], onehot[0:4, h, :], cpos[0:4, 0:1],
                                                         start=True, stop=True),
                          r=["fx_cpos", "consts"], w=[KA[1]])
                    P.add("pe", lambda e, h=h: e.matmul(psB[:, h * 9 + 1:h * 9 + 9], onehot[0:4, h, :],
                                                         cpos[0:4, 272:272 + 512 * 7 + 1:512], start=True, stop=True),
                          r=["fx_cpos", "consts"], w=[KA[1]])
                P.add("dve", lambda e: e.tensor_copy(out=cRb[:].rearrange("p h g -> p (h g)"), in_=psB[:, 0:36]),
                      r=[KA[1]], w=["fx_cRb"])
                for h in range(4):
                    for g in range(9):
                        P.add("dve", lambda e, h=h, g=g: e.tensor_scalar(out=biasF[:, h, g, :], in0=cK[:, :, h],
                                                                          scalar1=cRb[:, h, g:g + 1], scalar2=None,
                                                                          op0=ALU.subtract),
                              r=["fx_cK", "fx_cK0", "fx_cRb"], w=[("biasF", h)])
                for g, (c0, n) in enumerate(COLG):
                    rc = refcols[g]
                    P.add("dve", lambda e, c0=c0, n=n, rc=rc: e.tensor_scalar(out=cdiff[:, c0:c0 + n], in0=cpos[0:4, c0:c0 + n],
                                                                               scalar1=cpos[0:4, rc:rc + 1], scalar2=-1.0,
                                                                               op0=ALU.subtract, op1=ALU.mult),
                          r=["fx_cpos"], w=["cdiff"])
                P.barrier()

        PT = [sb(f"PT{i}", [128, 2, 512], BF16) for i in range(4)]
        kbctr = [0]

        def attention_unit(l, kind, h, mixst, mixst_key, nxt=None, post=None):
            fox = kind == "fox"
            biasF, cdiff = lay["biasF"], lay["cdiff"]
            ns = 1 if fox else 2
            GQ = 512 if fox else GQ_DIFF[h]
            nq = GQ // 128
            vw = 65 if fox else 129
            ve = 64 if fox else 128
            off = OFF_FOX[h] if fox else OFF_DIFF[h]
            ncols = 256 if fox else 512
            with contextlib.ExitStack() as st:
                QT = lay["QT"]
                KT = lay["KT"]
                VA = lay["VA"][:, :, 0:vw + 1]
                G = lay["G"][:, :, 0:ve]
                tz, o1, o2, sm, mtb = lay["tz"], lay["o1"], lay["o2"], lay["sm"], lay["mtb"]
                nS = 4 if fox else 2
                LA = nS - 1

                def S3(sl, nk, c0, c1):
                    if fox:
                        return psS[sl // 2][0:nk, sl % 2:sl % 2 + 1, c0:c1]
                    return psS[sl][0:nk, 0:2, c0:c1]

                def S2(sl, s, nk, c0, c1):
                    if fox:
                        return psS[sl // 2][0:nk, sl % 2, c0:c1]
                    return psS[sl][0:nk, s, c0:c1]

                def Skeys(sl):
                    return [("ps", sl)] if fox else [("ps", 2 * sl), ("ps", 2 * sl + 1)]
                deferred = lay["deferred"]
                check_sbuf("unit")
                W, wk = load_w(l, off, ncols)
                import os
                SK = os.environ.get("K_SKIP", "").split(",")
                if "ones" not in SK:
                    VAfull = lay["VA"]
                    P.add("pool", lambda e: e.memset(VAfull[:, :, :].rearrange("p b w -> p (b w)"), 2.0 if fox else 1.0), w=["VAone", "VA"])
                if fox:
                    P.add("pool", lambda e: e.memset(KT[64:65, :], 1.0), w=["KTone"])
                    P.add("pool", lambda e: e.memset(QT[64:128, :], 0.0), w=["QTaug"])
                    P.add("sp", lambda e: [e.dma_start(out=QT[64:65, :], in_=KT[96 + h:97 + h, :])], r=["cdiff"], w=["QTaug"],
                          dma=("qtaug", 1))
                for (c0, n) in (COLG if "qk" not in SK else []):
                    if fox:
                        ps, pk = next_sbank()
                        for c in range(8):
                            P.add("pe", lambda e, c=c, ps=ps, c0=c0, n=n: e.matmul(ps[:, 0:n], W[:, c, 0:128], hnT[:, c, c0:c0 + n],
                                                                                    start=(c == 0), stop=(c == 7)),
                                  r=[wk] + hnT_keys(c0, n), w=[pk])
                        P.add("act", lambda e, ps=ps, c0=c0, n=n: e.activation(out=QT[0:64, c0:c0 + n], in_=ps[0:64, 0:n], func=AF.Copy,
                                                                                scale=0.125), r=[pk], w=["QT"])
                        P.add("act", lambda e, ps=ps, c0=c0, n=n: e.activation(out=KT[0:64, c0:c0 + n], in_=ps[64:128, 0:n], func=AF.Copy),
                              r=[pk], w=["KT"])
                    else:
                        ps, pk = next_sbank()
                        for c in range(8):
                            P.add("pe", lambda e, c=c, ps=ps, c0=c0, n=n: e.matmul(ps[:, 0:n], W[:, c, 0:128], hnT[:, c, c0:c0 + n],
                                                                                    start=(c == 0), stop=(c == 7)),
                                  r=[wk] + hnT_keys(c0, n), w=[pk])
                        P.add("act", lambda e, ps=ps, c0=c0, n=n: e.activation(out=QT[:, c0:c0 + n], in_=ps[:, 0:n], func=AF.Copy,
                                                                                scale=0.125), r=[pk], w=["QT"])
                        ps2, pk2 = next_sbank()
                        for c in range(8):
                            P.add("pe", lambda e, c=c, ps2=ps2, c0=c0, n=n: e.matmul(ps2[:, 0:n], W[:, c, 128:256], hnT[:, c, c0:c0 + n],
                                                                                      start=(c == 0), stop=(c == 7)),
                                  r=[wk] + hnT_keys(c0, n), w=[pk2])
                        P.add("act", lambda e, ps2=ps2, c0=c0, n=n: e.activation(out=KT[:, c0:c0 + n], in_=ps2[:, 0:n], func=AF.Copy),
                              r=[pk2], w=["KT", "cdiff"])
                vz0 = 128 if fox else 256
                vzw = 2 * ve
                per = 512 // vzw
                batches = [[0]] + [list(range(1 + i, 1 + i + per)) for i in range(0, 32, per)]
                for bl in (batches if "vz" not in SK else []):
                    ps, pk = next_sbank()
                    psv = ps.rearrange("p (j w) -> p j w", w=vzw)
                    n = blk(bl[0])[1]
                    nb = len(bl)
                    for j, b in enumerate(bl):
                        c0 = blk(b)[0]
                        for c in range(8):
                            P.add("pe", lambda e, c=c, j=j, c0=c0, n=n, psv=psv: e.matmul(psv[0:n, j, :], hnT[:, c, c0:c0 + n],
                                                                                            W[:, c, vz0:vz0 + vzw], start=(c == 0), stop=(c == 7)),
                                  r=[wk, ("hnT", b)], w=[pk])
                    b0 = bl[0]
                    VZ = os.environ.get("K_VZ", "").split(",")
                    if "copy" not in VZ:
                        P.add("act", lambda e, psv=psv, n=n, nb=nb, b0=b0: e.activation(out=VA[0:n, b0:b0 + nb, 0:ve], in_=psv[0:n, 0:nb, 0:ve],
                                                                                         func=AF.Copy), r=[pk], w=["VA"])
                    if "tanh" in VZ:
                        continue
                    P.add("act", lambda e, psv=psv, n=n, nb=nb: e.activation(out=tz[0:n, 0:nb, 0:ve], in_=psv[0:n, 0:nb, ve:2 * ve],
                                                                              func=AF.Tanh, scale=0.5), r=[pk], w=["tz"])
                    P.add("dve", lambda e, psv=psv, n=n, nb=nb, b0=b0: e.scalar_tensor_tensor(out=G[0:n, b0:b0 + nb, :], in0=tz[0:n, 0:nb, 0:ve],
                                                                                              scalar=1.0, in1=psv[0:n, 0:nb, ve:2 * ve],
                                                                                              op0=ALU.add, op1=ALU.mult),
                          r=[pk, "tz"], w=["G"])

                RQ = ["QT", "QTaug"] if fox else ["QT"]
                RK = ["KT", "KTone", "cdiff"] if fox else ["KT"]
                krows = (lambda s: slice(0, 128)) if fox else (lambda s: slice(64 * s, 64 * s + 64))
                Mtab = maskc if fox else None

                def acc_region(s, t):
                    if fox:
                        return 0, t * 65
                    if GQ == 256:
                        return s, t * 129
                    if t < 3:
                        return s, t * 129
                    return 2, s * 129

                groups = [("meta", 0, 16, 0)] + [("x", 16 + GQ * gi, GQ, gi) for gi in range(4096 // GQ)]
                import os
                if os.environ.get("K_MAXG"):
                    groups = groups[int(os.environ.get("K_MING", "0")):int(os.environ["K_MAXG"])]
                for f_ in list(deferred):
                    f_()
                deferred.clear()
                pf_steps = []
                if nxt is not None and not os.environ.get("K_MAXG"):
                    Wn_, wkn_, pf_steps = load_w_steps(*nxt)
                    prefetched[(nxt[0], nxt[1])] = (Wn_, wkn_, pf_steps)
                for gidx_, (gk, qc0, qn, gi) in enumerate(groups):
                    if gk == "meta":
                        kblocks = [(0, 0, 16, True)]
                        nqt = 1
                    else:
                        t0 = gi * nq
                        kblocks = [(b, 0, GQ, False) for b in range(0, 1 + t0)]
                        kblocks += [(1 + t0 + kj, kj * 128, GQ - kj * 128, True) for kj in range(nq)]
                        nqt = nq
                    started = set()
                    nkb = len(kblocks)

                    def bias_ap(b, nk):
                        if fox:
                            g9 = 0 if gk == "meta" else 1 + gi
                            return biasF[0:nk, h, g9, b:b + 1]
                        if gk == "meta":
                            return ameta[0:nk, h, 0:1]
                        if b == 0:
                            return ameta[0:nk, h, 1 + gi:2 + gi]
                        d = (b - 1) - gi * nq
                        return atab[0:nk, h, d + 31:d + 32]

                    def emit_qk(i):
                        b, qoff, nqc, diag = kblocks[i]
                        kc0, nk = blk(b)
                        si = (kbctr[0] + i) % nS
                        for s in range(ns):
                            rs_ = krows(s)
                            P.add("pe", lambda e, s=s, rs_=rs_, si=si, nk=nk, kc0=kc0, qoff=qoff, nqc=nqc, qc0=qc0, diag=diag:
                                  e.matmul(S2(si, s, nk, qoff, qoff + nqc), KT[rs_, kc0:kc0 + nk],
                                           QT[rs_, qc0 + qoff:qc0 + qoff + nqc], start=True, stop=not diag,
                                           skip_group_check=diag),
                                  r=RQ + RK, w=[Skeys(si)[s]])
                        if diag and not fox:
                            dn = min(128, nqc)
                            if nk == 16:
                                combos = [(0, ident, mdiff_bf), (1, idsw_bf, mdsw_bf)]
                            else:
                                combos = [(0, ident, mdiff_bf), (1, ident, mdiff_bf), (0, idsw_bf, mdsw_bf), (1, idsw_bf, mdsw_bf)]
                            seen_ = {}
                            for ci, (s, idt, mt_) in enumerate(combos):
                                pr = slice(64 * s, 64 * s + 64)
                                is_last = all(c2[0] != s for c2 in combos[ci + 1:])
                                P.add("pe", lambda e, s=s, pr=pr, idt=idt, mt_=mt_, is_last=is_last, si=si, nk=nk, qoff=qoff, dn=dn:
                                      e.matmul(S2(si, s, nk, qoff, qoff + dn), idt[pr, 0:nk], mt_[pr, h, 0:dn],
                                               start=False, stop=is_last, skip_group_check=True),
                                      r=["ident", "mask_bf", "mask_bf2"], w=[Skeys(si)[s]])
                        if diag and fox:
                            dn = min(128, nqc)
                            if fox:
                                P.add("pe", lambda e, si=si, nk=nk, qoff=qoff, dn=dn:
                                      e.matmul(S2(si, 0, nk, qoff, qoff + dn), ident[0:nk, 0:nk], maskc_bf[0:nk, 0:dn],
                                               start=False, stop=True, skip_group_check=True),
                                      r=["ident", "mask_bf"], w=[Skeys(si)[0]])
                            elif nk == 16:
                                for s in range(ns):
                                    P.add("pe", lambda e, s=s, si=si, nk=nk, qoff=qoff, dn=dn:
                                          e.matmul(S2(si, s, nk, qoff, qoff + dn), ident[0:nk, 0:nk], mdiff_bf[0:nk, h, 0:dn],
                                                   start=False, stop=True, skip_group_check=True),
                                          r=["ident", "mask_bf"], w=[Skeys(si)[s]])
                            else:
                                for (s, hfp) in ((0, 0), (1, 1), (0, 1), (1, 0)):
                                    pr = slice(64 * hfp, 64 * hfp + 64)
                                    P.add("pe", lambda e, s=s, pr=pr, hfp=hfp, si=si, nk=nk, qoff=qoff, dn=dn:
                                          e.matmul(S2(si, s, nk, qoff, qoff + dn), ident[pr, 0:nk], mdiff_bf[pr, h, 0:dn],
                                                   start=False, stop=(hfp == 1 - s), skip_group_check=True),
                                          r=["ident", "mask_bf"], w=[Skeys(si)[s]])

                    def emit_rest(i):
                        b, qoff, nqc, diag = kblocks[i]
                        kc0, nk = blk(b)
                        si = (kbctr[0] + i) % nS
                        pi = (kbctr[0] + i) % 4
                        skeys = Skeys(si)
                        bap = bias_ap(b, nk)
                        P.add("act", lambda e: e.activation(out=PT[pi][0:nk, 0:ns, qoff:qoff + nqc], in_=S3(si, nk, qoff, qoff + nqc),
                                                            func=AF.Exp, bias=bap),
                              r=skeys + ["consts", ("biasF", h)] if fox else skeys + ["consts"], w=[("PT", pi)])
                        tq0 = qoff // 128
                        ntq = max(1, nqc // 128)
                        last = (i == nkb - 1)
                        for tt in range(ntq):
                            t = tq0 + tt
                            qn_t = min(128, nqc)
                            for s in range(ns):
                                bk, co = acc_region(s, t)
                                first = bk not in started
                                started.add(bk)
                                P.add("pe", lambda e, s=s, tt=tt, bk=bk, co=co, first=first, qn_t=qn_t:
                                      e.matmul(psA[0:qn_t, bk, co:co + vw], PT[pi][0:nk, s, qoff + tt * 128:qoff + tt * 128 + qn_t],
                                               VA[0:nk, b, 0:vw], start=first, stop=last, skip_group_check=True),
                                      r=[("PT", pi), "VA", "VAone"], w=[KA[bk]])

                    for i in range(min(LA, nkb)):
                        emit_qk(i)
                    for i in range(nkb):
                        if i + LA < nkb:
                            emit_qk(i + LA)
                        emit_rest(i)
                    kbctr[0] += nkb
                    if pf_steps and gidx_ in (0, 2, 5):
                        pf_steps.pop(0)()
                    for f_ in list(deferred):
                        f_()
                    deferred.clear()

                    for t in range(nqt):
                        n = 16 if gk == "meta" else 128
                        for s_ in range(ns):
                            bk, co = acc_region(s_, t)
                            P.add("dve", lambda e, t=t, s_=s_, bk=bk, co=co, n=n: e.tensor_copy(out=accS[0:n, t, s_, 0:vw], in_=psA[0:n, bk, co:co + vw]),
                                  r=[KA[bk]], w=[("accS", t, s_)])
                    for t in range(nqt):
                        n = 16 if gk == "meta" else 128
                        b = 0 if gk == "meta" else 1 + gi * nq + t
                        tc0 = blk(b)[0]
                        if fox:
                            P.add("dve", lambda e, t=t, n=n: e.reciprocal(out=sm[0:n, 0:1], in_=accS[0:n, t, 0, 64:65]),
                                  r=[("accS", t, 0)], w=["sm"])
                            P.add("dve", lambda e, t=t, n=n, b=b: e.scalar_tensor_tensor(out=mtb[t][0:n, 0:64], in0=accS[0:n, t, 0, 0:64],
                                                                                         scalar=sm[0:n, 0:1], in1=G[0:n, b, :],
                                                                                         op0=ALU.mult, op1=ALU.mult),
                                  r=[("accS", t, 0), "sm", "G"], w=[("mt", t)])
                        else:
                            kA_, kB_ = ("accS", t, 0), ("accS", t, 1)
                            P.add("dve", lambda e, n=n, t=t: e.reciprocal(out=sm[0:n, 0:2],
                                                                          in_=accS[0:n, t, 0:2, 128:129].rearrange("p s o -> p (s o)")),
                                  r=[kA_, kB_], w=["sm0", "sm1"])
                            P.add("dve", lambda e, n=n, t=t: e.tensor_scalar(out=o2[0:n, :], in0=accS[0:n, t, 1, 0:128], scalar1=sm[0:n, 1:2],
                                                                             scalar2=lamv[0:n, 5:6], op0=ALU.mult, op1=ALU.mult),
                                  r=[kB_, "sm1", "lamneg"], w=["o2"])
                            P.add("dve", lambda e, n=n, t=t: e.scalar_tensor_tensor(out=o1[0:n, :], in0=accS[0:n, t, 0, 0:128], scalar=sm[0:n, 0:1],
                                                                                    in1=o2[0:n, :], op0=ALU.mult, op1=ALU.add),
                                  r=[kA_, "sm0", "o2"], w=["o1"])
                            P.add("dve", lambda e, n=n: e.tensor_tensor(out=o2[0:n, :], in0=o1[0:n, :], in1=o1[0:n, :], op=ALU.mult),
                                  r=["o1"], w=["o2"])
                            P.add("dve", lambda e, n=n: e.tensor_reduce(out=sm[0:n, 3:4], in_=o2[0:n, :], axis=AX.X, op=ALU.add),
                                  r=["o2"], w=["sm3"])
                            P.add("dve", lambda e, n=n: e.tensor_scalar(out=sm[0:n, 4:5], in0=sm[0:n, 3:4], scalar1=1.0 / 128, scalar2=1e-5,
                                                                         op0=ALU.mult, op1=ALU.add), r=["sm3"], w=["sm4"])
                            P.add("pool", lambda e, n=n: e.tensor_tensor(out=sm[0:n, 5:6], in0=sm[0:n, 4:5], in1=cneg[0:n, 0:1], op=ALU.pow),
                                  r=["sm4", "cneg"], w=["sm5"])
                            P.add("pool", lambda e, n=n, b=b: e.tensor_tensor(out=o2[0:n, :], in0=G[0:n, b, :], in1=subg2[0:n, :], op=ALU.mult),
                                  r=["G", "subg2", "sm3"], w=["o2"])
                            P.add("dve", lambda e, n=n, t=t: e.scalar_tensor_tensor(out=mtb[t][0:n, :], in0=o1[0:n, :], scalar=sm[0:n, 5:6], in1=o2[0:n, :],
                                                                                     op0=ALU.mult, op1=ALU.mult), r=["o1", "sm5", "o2"], w=[("mt", t)])

                    def flush_T(nqt=nqt, n=(16 if gk == "meta" else 128), tc0=blk(0 if gk == "meta" else 1 + gi * nq)[0]):
                        ew = 64 if fox else 128
                        hh = (h % 2) * 64 if fox else 0
                        for t in range(nqt):
                            P.add("pe", lambda e, t=t: e.transpose(psT[0:ew, t * 128:t * 128 + n], mtb[t][0:n, 0:ew], ident[0:n, 0:n]),
                                  r=[("mt", t), "ident"], w=[KT_])
                        ncol = n if nqt == 1 else nqt * 128
                        P.add("dve", lambda e: e.tensor_copy(out=mixst[hh:hh + ew, tc0:tc0 + ncol], in_=psT[0:ew, 0:ncol]),
                              r=[KT_], w=[mixst_key])
                    deferred.append(flush_T)
                if post is not None:
                    deferred.append(post)

        def pool_unit(l, mixst1, flush):
            with contextlib.ExitStack() as st:
                LV = [sb(f"p_lv{i}", [128, 528], F32, st) for i in range(5)]
                pbf2 = [sb(f"p_pbf{i}", [128, 512], BF16, st) for i in range(2)]
                ptz2 = [sb(f"p_tz{i}", [128, 512], F32, st) for i in range(2)]
                pgz2 = [sb(f"p_gz{i}", [128, 512], F32, st) for i in range(2)]
                ptmp = sb("p_tmp", [128, 16], F32, st)
                W, wk = load_w(l, OFF_POOL, 512)
                def pool_half(hf, mixst, mkey):
                    klev = (1, 2) if hf == 0 else (3, 4)
                    kmax = klev[1]
                    for i in range(5):
                        P.add("pool", lambda e, i=i: e.memset(LV[i][:, 0:16], 0.0), w=[("lv", i)])

                    def stage1(gidx, c0, n):
                        pp = gidx % 2
                        pbf_, ptz_, pgz_ = pbf2[pp], ptz2[pp], pgz2[pp]
                        psu, ku = next_sbank()
                        psz, kz = next_sbank()
                        for c in range(8):
                            P.add("pe", lambda e, c=c: e.matmul(psu[:, 0:n], W[:, c, hf * 128:(hf + 1) * 128], hnT[:, c, c0:c0 + n],
                                                                 start=(c == 0), stop=(c == 7)),
                                  r=[wk] + hnT_keys(c0, n), w=[ku])
                        for c in range(8):
                            P.add("pe", lambda e, c=c: e.matmul(psz[:, 0:n], W[:, c, 256 + hf * 128:256 + (hf + 1) * 128],
                                                                 hnT[:, c, c0:c0 + n], start=(c == 0), stop=(c == 7)),
                                  r=[wk] + hnT_keys(c0, n), w=[kz])
                        P.add("dve", lambda e: e.tensor_copy(out=LV[0][:, 16:16 + n], in_=psu[:, 0:n]), r=[ku], w=[("lv", 0)])
                        P.add("act", lambda e: e.activation(out=ptz_[:, 0:n], in_=psz[:, 0:n], func=AF.Tanh, scale=0.5),
                              r=[kz], w=[("ptz", pp)])
                        P.add("dve", lambda e: e.scalar_tensor_tensor(out=pgz_[:, 0:n], in0=ptz_[:, 0:n], scalar=1.0, in1=psz[:, 0:n],
                                                                      op0=ALU.add, op1=ALU.mult), r=[kz, ("ptz", pp)], w=[("pgz", pp)])
                        for i in range(1, kmax + 1):
                            sh = 1 << (i - 1)
                            P.add("pool", lambda e, i=i, sh=sh: e.tensor_tensor(out=LV[i][:, 16:16 + n], in0=LV[i - 1][:, 16:16 + n],
                                                                                in1=LV[i - 1][:, 16 - sh:16 - sh + n], op=ALU.add),
                                  r=[("lv", i - 1)], w=[("lv", i)])
                        for pr in range(2):
                            k = klev[pr]
                            rows = slice(64 * pr, 64 * pr + 64)
                            if c0 == 0:
                                P.add("dve", lambda e, k=k, rows=rows: e.tensor_tensor(out=ptmp[rows, 0:16], in0=LV[k][rows, 16:32],
                                                                                       in1=invc[rows, hf, :], op=ALU.mult),
                                      r=[("lv", k), "consts"], w=["ptmp"])
                                P.add("dve", lambda e, rows=rows: e.tensor_tensor(out=pbf_[rows, 0:16], in0=ptmp[rows, 0:16], in1=LV[0][rows, 16:32],
                                                                                  op=ALU.subtract), r=["ptmp", ("lv", 0)], w=[("pbf", pp)])
                            else:
                                P.add("dve", lambda e, k=k, rows=rows: e.scalar_tensor_tensor(out=pbf_[rows, 0:n], in0=LV[k][rows, 16:16 + n],
                                                                                              scalar=1.0 / (1 << k), in1=LV[0][rows, 16:16 + n],
                                                                                              op0=ALU.mult, op1=ALU.subtract),
                                      r=[("lv", k), ("lv", 0)], w=[("pbf", pp)])
                        for i in range(0, kmax + 1):
                            P.add("pool", lambda e, i=i: e.tensor_copy(out=LV[i][:, 0:16], in_=LV[i][:, n:n + 16]),
                                  r=[("lv", i)], w=[("lv", i)])

                    def stage2(gidx, c0, n):
                        pp = gidx % 2
                        pbf_, pgz_ = pbf2[pp], pgz2[pp]
                        psy, ky = next_sbank()
                        P.add("pe", lambda e: e.matmul(psy[:, 0:n], bd[:, hf, :], pbf_[:, 0:n], start=True, stop=True),
                              r=["bd", ("pbf", pp)], w=[ky])
                        P.add("dve", lambda e: e.scalar_tensor_tensor(out=mixst[:, c0:c0 + n], in0=psy[:, 0:n],
                                                                      scalar=pscale2[:, hf:hf + 1], in1=pgz_[:, 0:n],
                                                                      op0=ALU.mult, op1=ALU.mult),
                              r=[ky, ("pgz", pp), "pscale2"], w=[mkey])

                    stage1(0, *COLG[0])
                    for gidx, (c0, n) in enumerate(COLG):
                        if gidx + 1 < len(COLG):
                            stage1(gidx + 1, *COLG[gidx + 1])
                        stage2(gidx, c0, n)

                check_sbuf("pool")
                for hf_ in range(2):
                    pool_half(hf_, mixst1, ("mixst", 0))
                    flush(6 + hf_, mixst1, ("mixst", 0))
                P.barrier()

        def phase_c(l, last):
            with contextlib.ExitStack() as st:
                nb_ = dict(junk=[sb(f"c_junk{i}", [128, D], BF16, st) for i in range(2)],
                           ss=[sb(f"c_ss{i}", [128, 2], F32, st) for i in range(2)],
                           rs=[sb(f"c_rs{i}", [128, 1], F32, st) for i in range(2)],
                           hb=[sb(f"c_hb{i}", [128, D], BF16, st) for i in range(2)])
                wo = sb("c_wo", [128, 8, D], BF16, st)
                mixin = [sb(f"c_mix{i}", [128, 8, 512], BF16, st) for i in range(2)]
                hold = [sb(f"c_hold{i}", [128, D], F32, st) for i in range(2)]
                hnew = [sb(f"c_hnew{i}", [128, D], F32, st) for i in range(2)]
                yo = hold
                check_sbuf("phasec")
                for c in range(8):
                    ss = stctr[0] % 2
                    stctr[0] += 1
                    P.add("sp", lambda e, c=c, ss=ss: [e.dma_start(out=wstage[ss][:, :], in_=wout_d[l, :, c, :])],
                          w=[("wst", ss)], dma=(f"wst{ss}", 1))
                    P.add("act", lambda e, c=c, ss=ss: e.activation(out=wo[:, c, :], in_=wstage[ss][:, :], func=AF.Copy),
                          r=[("wst", ss)], w=[("wo", c)])
                load_g(fing_d if last else normg_d[l + 1:l + 2, :])
                if not last and units is None:
                    prefetch_w(l + 1, OFF_FOX[0], 256)
                def part1(gidx, gc0, gn, b, first):
                    ms = gidx % 2
                    if first:
                        P.add("sp", lambda e: [e.dma_start(out=mixin[ms][:, :, 0:gn],
                                                           in_=mixT_d[:, :, gc0:gc0 + gn].rearrange("c p t -> p c t"))],
                              r=[("mixT_d", c) for c in range(8)], w=[("mixin", ms)], dma=(f"mixin{ms}", 1))
                    c0, n = blk(b)
                    lo = c0 - gc0
                    hs = b % 2
                    if l == 0:
                        src = meta_d if b == 0 else x_d[(b - 1) * 128:b * 128, :]
                        rk = []
                    else:
                        src = h_d[c0:c0 + n, :]
                        rk = [("h_d", b)]
                    P.add("sp", lambda e: [e.dma_start(out=hold[hs][0:n, :], in_=src)],
                          r=rk, w=[("hold", hs)], dma=(f"hold{hs}", 1))
                    si = b % 2
                    for half in range(2):
                        for c in range(8):
                            P.add("pe", lambda e, c=c, half=half:
                                  e.matmul(psS[si][0:n, half, :], mixin[ms][:, c, lo:lo + n], wo[:, c, half * 512:(half + 1) * 512],
                                           start=(c == 0), stop=(c == 7)),
                                  r=[("mixin", ms), ("wo", c)], w=[("ps", 2 * si + half)])

                def part2(b):
                    c0, n = blk(b)
                    hs = b % 2
                    si = b % 2
                    P.add("dve", lambda e: e.tensor_tensor(out=hnew[hs][0:n, :].rearrange("p (a f) -> p a f", a=2),
                                                           in0=psS[si][0:n, :, :],
                                                           in1=hold[hs][0:n, :].rearrange("p (a f) -> p a f", a=2), op=ALU.add),
                          r=[("ps", 2 * si), ("ps", 2 * si + 1), ("hold", hs)], w=[("hnew", hs)])
                    if not last:
                        P.add("sp", lambda e: [e.dma_start(out=h_d[c0:c0 + n, :], in_=hnew[hs][0:n, :])],
                              r=[("hnew", hs)], w=[("h_d", b)], dma=(f"hout{hs}", 1))
                        norm_tile(b, hnew[hs], ("hnew", hs), nb_)
                        return
                    if dbg:
                        P.add("sp", lambda e: [e.dma_start(out=h_d[c0:c0 + n, :], in_=hnew[hs][0:n, :])],
                              r=[("hnew", hs)], w=[("h_d", b)], dma=(f"hout{hs}", 1))
                    if b == 0:
                        return
                    pb_ = b % 2
                    jk, ss, rs = nb_["junk"][pb_], nb_["ss"][pb_], nb_["rs"][pb_]
                    P.add("act", lambda e: e.activation(out=jk[:, :], in_=hnew[hs][:, :], func=AF.Square, accum_out=ss[:, 0:1]),
                          r=[("hnew", hs)], w=[("nt_junk", pb_), ("nt_ss", pb_)])
                    P.add("dve", lambda e: e.tensor_scalar(out=ss[:, 1:2], in0=ss[:, 0:1], scalar1=1.0 / D, scalar2=1e-6,
                                                            op0=ALU.mult, op1=ALU.add), r=[("nt_ss", pb_)], w=[("nt_ss2", pb_)])
                    P.add("pool", lambda e: e.tensor_tensor(out=rs[:, 0:1], in0=ss[:, 1:2], in1=cneg[:, 0:1], op=ALU.pow),
                          r=[("nt_ss2", pb_), "cneg"], w=[("nt_rs", pb_)])
                    P.add("dve", lambda e: e.scalar_tensor_tensor(out=yo[hs][:, :], in0=hnew[hs][:, :], scalar=rs[:, 0:1],
                                                                  in1=g_bc[:, :], op0=ALU.mult, op1=ALU.mult),
                          r=[("hnew", hs), ("nt_rs", pb_), "g_bc"], w=[("hold", hs)])
                    P.add("sp", lambda e: [e.dma_start(out=y_d[(b - 1) * 128:b * 128, :], in_=yo[hs][:, :])],
                          r=[("hold", hs)], w=[("y", b)], dma=(f"yout{hs}", 1))

                tiles = []
                for gidx, (gc0, gn) in enumerate(COLG):
                    blocks = [0] if gidx == 0 else list(range(1 + 4 * (gidx - 1), 5 + 4 * (gidx - 1)))
                    for j, b in enumerate(blocks):
                        tiles.append((gidx, gc0, gn, b, j == 0))
                part1(*tiles[0])
                for i in range(len(tiles)):
                    if i + 1 < len(tiles):
                        part1(*tiles[i + 1])
                    part2(tiles[i][3])
                P.barrier()

        phase_a()
        for l in range(nlayers):
            P.layer = l
            last = (l == nlayers - 1)
            layer_prep(l)
            ulist = units if units is not None else ["fox", "diff", "pool"]
            with contextlib.ExitStack() as ust:
                mixst1 = sb("mixst", [128, T], BF16, ust)
                mixst = [mixst1, mixst1]
                lay["biasF"] = sb("biasF", [128, 4, 9, 33], F32, ust)
                lay["QT"] = sb("u_QT", [128, T], BF16, ust)
                lay["KT"] = sb("u_KT", [128, T], BF16, ust)
                lay["cdiff"] = lay["KT"][96:100, :]
                mctr = 0

                def flush(chunk, buf, key):
                    P.add("sp", lambda e: [e.dma_start(out=mixT_d[chunk], in_=buf[:, :])], r=[key], w=[("mixT_d", chunk)],
                          dma=("mixout", 1))

                if "fox" in ulist:
                    fox_prep(l)
                with contextlib.ExitStack() as ust2:
                    lay["VA"] = sb("u_VA", [128, NB, 130], BF16, ust2)
                    lay["G"] = sb("u_G", [128, NB, 128], BF16, ust2)
                    lay["tz"] = sb("u_tz", [128, 4, 128], F32, ust2)
                    lay["o1"] = sb("u_o1", [128, 128], F32, ust2)
                    lay["o2"] = sb("u_o2", [128, 128], F32, ust2)
                    lay["sm"] = sb("u_sm", [128, 8], F32, ust2)
                    lay["mtb"] = [sb(f"u_mt{i}", [128, 128], BF16, ust2) for i in range(4)]
                    lay["deferred"] = []
                    if "fox" in ulist:
                        for h in range(4):
                            i = (h // 2) % 2
                            nxt = (l, OFF_FOX[h + 1], 256) if h < 3 else ((l, OFF_DIFF[0], 512) if "diff" in ulist else None)
                            post = (lambda h=h, i=i: flush(h // 2, mixst[i], ("mixst", 0))) if h % 2 == 1 else None
                            attention_unit(l, "fox", h, mixst[i], ("mixst", 0), nxt, post)
                    if "diff" in ulist:
                        for h in range(4):
                            i = h % 2
                            nxt = (l, OFF_DIFF[h + 1], 512) if h < 3 else ((l, OFF_POOL, 512) if "pool" in ulist else None)
                            post = (lambda h=h, i=i: flush(2 + h, mixst[i], ("mixst", 0)))
                            attention_unit(l, "diff", h, mixst[i], ("mixst", 0), nxt, post)
                    for f_ in list(lay["deferred"]):
                        f_()
                    lay["deferred"].clear()
                    P.barrier()
                if "pool" in ulist:
                    pool_unit(l, mixst1, flush)
                P.barrier()
            phase_c(l, last)
        allkeys = [k for k in list(P.res.keys())]
        P.add("sp", lambda e: e.nop() if hasattr(e, "nop") else None, r=[], w=allkeys)

        P.finalize()
        build_program.peak_sbuf = peak[0]
        with nc.Block() as block:
            @block.tensor
            def _(e):
                P.emit("pe", e)

            @block.scalar
            def _(e):
                P.emit("act", e)

            @block.vector
            def _(e):
                P.emit("dve", e)

            @block.gpsimd
            def _(e):
                P.emit("pool", e)

            @block.sync
            def _(e):
                P.emit("sp", e)
    return nc


def make_consts():
    kk = np.arange(128)[:, None].astype(np.float64)
    qq = np.arange(128)[None, :].astype(np.float64)
    maskc = np.where(qq >= kk, 0.0, NEGM).astype(np.float32)
    mdiff = np.zeros((128, 4, 128), np.float32)
    atab = np.zeros((128, 4, 35), np.float32)
    ameta = np.zeros((128, 4, 17), np.float32)
    for h in range(4):
        s = SLOPES[h]
        m = np.where(qq >= kk, 0.0, -2.0 * s * (kk - qq))
        m = m + np.where((kk >= 64) & (qq < 64), NEGM, 0.0)
        mdiff[:, h, :] = m
        GQ = GQ_DIFF[h]
        for di in range(35):
            d = di - 31
            atab[:, h, di] = s * (128.0 * d + kk[:, 0] - GQ / 2)
        ameta[:, h, 0] = s * kk[:, 0]
        for gi in range(4096 // GQ):
            ameta[:, h, 1 + gi] = s * (kk[:, 0] - (16 + GQ * gi + GQ / 2))
    onehot = np.zeros((4, 4, 128), np.float32)
    for h in range(4):
        onehot[h, h, :] = 1.0
    invc = np.zeros((128, 2, 16), np.float32)
    for hf in range(2):
        for pr in range(2):
            w = 2 ** ((1, 2)[pr] if hf == 0 else (3, 4)[pr])
            t = np.arange(16)
            invc[64 * pr:64 * pr + 64, hf, :] = (1.0 / np.minimum(t + 1, w))[None, :]
    idsw = np.roll(np.eye(128, dtype=np.float32), 64, axis=1)
    mdsw = np.ascontiguousarray(np.roll(mdiff, -64, axis=0))
    return dict(c_maskc=maskc, c_mdiff=mdiff, c_atab=atab, c_ameta=ameta, c_onehot=onehot, c_invc=invc,
                c_idsw=idsw, c_mdiffsw=mdsw)


def relayout_w_in(w_in):
    cols = []
    for h in range(4):
        cols += list(range(h * 64, h * 64 + 64))
        cols += list(range(256 + h * 64, 256 + h * 64 + 64))
        cols += list(range(512 + h * 64, 512 + h * 64 + 64))
        cols += list(range(768 + h * 64, 768 + h * 64 + 64))
    for h in range(4):
        cols += list(range(1028 + h * 128, 1028 + h * 128 + 128))
        cols += list(range(1540 + h * 128, 1540 + h * 128 + 128))
        cols += list(range(2052 + h * 128, 2052 + h * 128 + 128))
        cols += list(range(2564 + h * 128, 2564 + h * 128 + 128))
    cols += list(range(3076, 3076 + 256))
    cols += list(range(3332, 3332 + 256))
    cols += list(range(1024, 1028))
    w = w_in[:, :, np.asarray(cols)]
    w = w.reshape(w.shape[0], 8, 128, 3588).transpose(0, 2, 1, 3)
    return np.ascontiguousarray(w)


_CACHE = {}


def prep_inputs(x, meta_tokens, norm_g, w_in, b_f, lam_q1, lam_k1, lam_q2, lam_k2, subln_g, w_pool, pool_scale,
                w_out, final_g):
    f = lambda a: np.ascontiguousarray(np.asarray(a, dtype=np.float32))
    shared = dict(
        meta=f(meta_tokens), norm_g=f(norm_g), w_in=relayout_w_in(f(w_in)),
        b_f=f(b_f).reshape(L, 4, 1),
        lam=np.ascontiguousarray(np.concatenate([f(lam_q1), f(lam_k1), f(lam_q2), f(lam_k2)], axis=1)),
        subln_g=f(subln_g), w_pool=f(w_pool),
        pool_scale=np.ascontiguousarray(f(pool_scale).reshape(L, 2, 128).transpose(0, 2, 1)),
        w_out=np.ascontiguousarray(f(w_out).reshape(L, 8, 128, D).transpose(0, 2, 1, 3)),
        final_g=f(final_g).reshape(1, D),
    )
    shared.update(make_consts())
    xs = f(x)
    return [dict(shared, x=np.ascontiguousarray(xs[i])) for i in range(xs.shape[0])]


def kernel(**inputs):
    in_maps = prep_inputs(**inputs)
    if "nc" not in _CACHE:
        _CACHE["nc"] = build_program()
    nc = _CACHE["nc"]
    res = run_bass_kernel_spmd(nc, in_maps, core_ids=list(range(8)))
    return np.stack([np.asarray(r["y"]).reshape(S_LEN, D) for r in res.results], axis=0).astype(np.float32)
```

```python
import contextlib
import numpy as np
import ml_dtypes
import concourse.bass as bass
import concourse.mybir as mybir
from concourse.bass_utils import run_bass_kernel_spmd

F32 = mybir.dt.float32
BF16 = mybir.dt.bfloat16
AF = mybir.ActivationFunctionType
ALU = mybir.AluOpType
AX = mybir.AxisListType

L = 4
D = 1024
S_LEN = 4096
T = 4112
NB = 33
SLOPES = [2.0 ** (-8.0 * (h + 1) / 4) for h in range(4)]
NEGM = -30000.0
COLG = [(0, 16)] + [(16 + 512 * i, 512) for i in range(8)]
OFF_FOX = [256 * h for h in range(4)]
OFF_DIFF = [1024 + 512 * h for h in range(4)]
OFF_POOL = 3072
OFF_FG = 3584
GQ_DIFF = [256, 512, 512, 512]


def blk(b):
    return (0, 16) if b == 0 else (16 + 128 * (b - 1), 128)


class Op:
    __slots__ = ("eng", "fn", "deps", "inc", "val", "dma", "layer")


class DSem:
    __slots__ = ("h", "total", "last")


ENGS = ("pe", "act", "dve", "pool", "sp")


class Prog:
    def __init__(self, nc, es, nlayers):
        self.nc = nc
        self.es = es
        self.ops = {e: [] for e in ENGS}
        self.res = {}
        self.layer = 0
        self.esem = {}
        for e in ("pe", "act", "dve", "pool", "sp"):
            for l in range(nlayers):
                self.esem[(e, l)] = es.enter_context(nc.semaphore(f"s_{e}_{l}"))
        self.dsem = {}
        self.pending = {}

    def barrier(self):
        keys = list(self.res.keys())
        o = None
        for e in ENGS:
            o = self.add(e, lambda en: en.nop(), w=keys, force=True)
        self.pending = {e: o for e in ENGS}

    def _ds(self, name):
        d = self.dsem.get(name)
        if d is None:
            d = DSem()
            d.h = self.es.enter_context(self.nc.semaphore("d_" + name))
            d.total = 0
            d.last = None
            self.dsem[name] = d
        return d

    def add(self, eng, fn, r=(), w=(), dma=None, force=False):
        o = Op()
        o.eng = eng
        o.fn = fn
        o.dma = dma
        o.layer = self.layer
        o.inc = False
        o.val = None
        deps = []
        w = list(w) + [k for k in r if isinstance(k, tuple) and k[0] == "ps" and k not in w]
        for k in r:
            st = self.res.get(k)
            if st is not None and st[0] is not None:
                deps.append(st[0])
        for k in w:
            st = self.res.get(k)
            if st is not None:
                if st[0] is not None:
                    deps.append(st[0])
                deps.extend(st[1].values())
                deps.extend(st[2])
        if dma is not None:
            ds = self._ds(dma[0])
            if ds.last is not None:
                deps.append(ds.last)
            ds.last = o
        pb = self.pending.pop(eng, None)
        if pb is not None:
            deps.append(pb)
        o.deps = []
        seen = set()
        for d in deps:
            if d is o or id(d) in seen:
                continue
            seen.add(id(d))
            if d.dma is None and o.dma is None and d.eng == eng and eng == "pe" and not force:
                continue
            o.deps.append(d)
            d.inc = True
        for k in r:
            st = self.res.get(k)
            if st is None:
                st = [None, {}, []]
                self.res[k] = st
            if dma is not None:
                st[2].append(o)
            else:
                st[1][eng] = o
        for k in w:
            self.res[k] = [o, {}, []]
        self.ops[eng].append(o)
        return o

    def finalize(self):
        cnt = {}
        for e in ENGS:
            for o in self.ops[e]:
                if o.dma is not None:
                    ds = self.dsem[o.dma[0]]
                    ds.total += 16 * o.dma[1]
                    o.val = (ds.h, ds.total, "d_" + o.dma[0])
                elif o.inc:
                    k = (e, o.layer)
                    cnt[k] = cnt.get(k, 0) + 1
                    o.val = (self.esem[k], cnt[k], f"{e}{o.layer}")

    def emit(self, eng, e):
        waited = {}
        for o in self.ops[eng]:
            for d in o.deps:
                sem, val, name = d.val
                if waited.get(name, 0) >= val:
                    continue
                e.wait_ge(sem, val)
                waited[name] = val
            ins = o.fn(e)
            if o.dma is not None:
                sem = o.val[0]
                assert len(ins) == o.dma[1], (len(ins), o.dma)
                for i in ins:
                    i.then_inc(sem, 16)
            elif o.inc:
                ins.then_inc(o.val[0], 1)


def build_program(nlayers=L, dbg=False, units=None):
    nc = bass.Bass("TRN2", target_bir_lowering=False)
    es = contextlib.ExitStack()
    with es:
        kin = "ExternalInput"
        x_d = nc.dram_tensor("x", [S_LEN, D], F32, kind=kin).ap()
        meta_d = nc.dram_tensor("meta", [16, D], F32, kind=kin).ap()
        normg_d = nc.dram_tensor("norm_g", [L, D], F32, kind=kin).ap()
        win_d = nc.dram_tensor("w_in", [L, 128, 8, 3588], F32, kind=kin).ap()
        bf_d = nc.dram_tensor("b_f", [L, 4, 1], F32, kind=kin).ap()
        lam_d = nc.dram_tensor("lam", [L, 256], F32, kind=kin).ap()
        subg_d = nc.dram_tensor("subln_g", [L, 128], F32, kind=kin).ap()
        wpool_d = nc.dram_tensor("w_pool", [L, 4, 64, 64], F32, kind=kin).ap()
        pscale_d = nc.dram_tensor("pool_scale", [L, 128, 2], F32, kind=kin).ap()
        wout_d = nc.dram_tensor("w_out", [L, 128, 8, D], F32, kind=kin).ap()
        fing_d = nc.dram_tensor("final_g", [1, D], F32, kind=kin).ap()
        maskc_d = nc.dram_tensor("c_maskc", [128, 128], F32, kind=kin).ap()
        mdiff_d = nc.dram_tensor("c_mdiff", [128, 4, 128], F32, kind=kin).ap()
        atab_d = nc.dram_tensor("c_atab", [128, 4, 35], F32, kind=kin).ap()
        ameta_d = nc.dram_tensor("c_ameta", [128, 4, 17], F32, kind=kin).ap()
        onehot_d = nc.dram_tensor("c_onehot", [4, 4, 128], F32, kind=kin).ap()
        invc_d = nc.dram_tensor("c_invc", [128, 2, 16], F32, kind=kin).ap()
        idsw_d = nc.dram_tensor("c_idsw", [128, 128], F32, kind=kin).ap()
        mdsw_d = nc.dram_tensor("c_mdiffsw", [128, 4, 128], F32, kind=kin).ap()
        y_d = nc.dram_tensor("y", [S_LEN, D], F32, kind="ExternalOutput").ap()
        skind = "ExternalOutput" if dbg else "Internal"
        h_d = nc.dram_tensor("h_scr", [T, D], F32, kind=skind).ap()
        mixT_d = nc.dram_tensor("mixT_scr", [8, 128, T], BF16, kind=skind).ap()

        uniq = [0]

        def sb(name, shape, dt, st=es):
            uniq[0] += 1
            return st.enter_context(nc.sbuf_tensor(f"{name}_{uniq[0]}", shape, dt))

        P = Prog(nc, es, nlayers)
        peak = [0]

        def check_sbuf(tag):
            return
            try:
                with nc.sbuf_tensor(f"probe_{tag}_{uniq[0]}", [128, 400000], F32):
                    pass
            except AssertionError as ex:
                msg = str(ex)
                have = int(msg.split("have ")[1].split("B")[0])
                top = int(msg.split("top=")[1].split(")")[0])
                used = top - have
                peak[0] = max(peak[0], used)
                assert used <= 192 * 1024 - 256, f"SBUF over budget at {tag}: {used}"

        psS = [es.enter_context(nc.psum_tensor(f"psS{i}", [128, 2, 512], F32)) for i in range(2)]
        psA = es.enter_context(nc.psum_tensor("psA", [128, 3, 512], F32))
        psT = es.enter_context(nc.psum_tensor("psT", [128, 1024], BF16))

        def sbank(i):
            return psS[i // 2][:, i % 2, :], ("ps", i)

        KA = [("ps", 4), ("ps", 5), ("ps", 6)]
        KT_ = ("ps", 7)
        rot = [0]

        def next_sbank():
            i = rot[0] % 4
            rot[0] += 1
            return sbank(i)

        hnT = sb("hnT", [128, 8, T], BF16)
        ident = sb("ident", [128, 128], BF16)
        identf = sb("identf", [128, 128], F32)
        maskc = sb("maskc", [128, 128], F32)
        mdiff = sb("mdiff", [128, 4, 128], F32)
        atab = sb("atab", [128, 4, 35], F32)
        ameta = sb("ameta", [128, 4, 17], F32)
        onehot = sb("onehot", [4, 4, 128], F32)
        invc = sb("invc", [128, 2, 16], F32)
        cneg = sb("cneg", [128, 2], F32)
        g_bc = sb("g_bc", [128, D], F32)
        lam_bc = sb("lam_bc", [128, 256], F32)
        lamv = sb("lamv", [128, 8], F32)
        subg = sb("subg", [128, 128], F32)
        subg2 = sb("subg2", [128, 128], F32)
        bfcol = sb("bfcol", [4, 2], F32)
        pscale = sb("pscale", [128, 2], F32)
        pscale2 = sb("pscale2", [128, 2], F32)
        wp_stage = sb("wp_stage", [128, 2, 128], F32)
        bd = sb("bd", [128, 2, 128], BF16)
        lay = {}
        junk64 = sb("junk64", [128, 64], F32)
        maskc_bf = sb("maskc_bf", [128, 128], BF16)
        mdiff_bf = sb("mdiff_bf", [128, 4, 128], BF16)
        mdsw_bf = sb("mdsw_bf", [128, 4, 128], BF16)
        idsw_bf = sb("idsw_bf", [128, 128], BF16)

        def f_const(e):
            return [
                e.dma_start(out=maskc[:], in_=maskc_d),
                e.dma_start(out=mdiff[:], in_=mdiff_d),
                e.dma_start(out=atab[:], in_=atab_d),
                e.dma_start(out=ameta[:], in_=ameta_d),
                e.dma_start(out=onehot[:], in_=onehot_d),
                e.dma_start(out=invc[:], in_=invc_d),
            ]

        P.add("sp", f_const, w=["consts"], dma=("const", 6))
        P.add("pool", lambda e: e.memset(ident[:], 0.0), w=["ident"])
        P.add("pool", lambda e: e.affine_select(out=ident[:], in_=ident[:], pattern=[[-1, 128]],
                                                 compare_op=ALU.not_equal, fill=1.0, base=0,
                                                 channel_multiplier=1), r=["ident"], w=["ident"])
        P.add("pool", lambda e: e.memset(identf[:], 0.0), w=["identf"])
        P.add("pool", lambda e: e.affine_select(out=identf[:], in_=identf[:], pattern=[[-1, 128]],
                                                 compare_op=ALU.not_equal, fill=1.0, base=0,
                                                 channel_multiplier=1), r=["identf"], w=["identf"])
        P.add("pool", lambda e: e.memset(cneg[:, 0:1], -0.5), w=["cneg"])
        P.add("pool", lambda e: e.tensor_copy(out=maskc_bf[:], in_=maskc[:]), r=["consts"], w=["mask_bf"])
        P.add("pool", lambda e: e.tensor_copy(out=mdiff_bf[:], in_=mdiff[:]), r=["consts"], w=["mask_bf"])
        P.add("sp", lambda e: [e.dma_start(out=mdiff[:], in_=mdsw_d)], r=["mask_bf"], w=["consts2"], dma=("const2", 1))
        P.add("pool", lambda e: e.tensor_copy(out=mdsw_bf[:], in_=mdiff[:]), r=["consts2"], w=["mask_bf2"])
        P.add("sp", lambda e: [e.dma_start(out=maskc[:], in_=idsw_d)], r=["mask_bf"], w=["consts3"], dma=("const3", 1))
        P.add("pool", lambda e: e.tensor_copy(out=idsw_bf[:], in_=maskc[:]), r=["consts3"], w=["mask_bf2"])

        def norm_tile(b, h_sb, hkey, nb_):
            c0, n = blk(b)
            pb_ = b % 2
            jk, ss, rs, hb = nb_["junk"][pb_], nb_["ss"][pb_], nb_["rs"][pb_], nb_["hb"][b % 2]
            hbk = ("hb", b % 2)
            P.add("act", lambda e: e.activation(out=jk[0:n, :], in_=h_sb[0:n, :], func=AF.Square,
                                                 accum_out=ss[0:n, 0:1]), r=[hkey], w=[("nt_junk", pb_), ("nt_ss", pb_)])
            P.add("dve", lambda e: e.tensor_scalar(out=ss[0:n, 1:2], in0=ss[0:n, 0:1], scalar1=1.0 / D,
                                                    scalar2=1e-6, op0=ALU.mult, op1=ALU.add),
                  r=[("nt_ss", pb_)], w=[("nt_ss2", pb_)])
            P.add("pool", lambda e: e.tensor_tensor(out=rs[0:n, 0:1], in0=ss[0:n, 1:2], in1=cneg[0:n, 0:1],
                                                     op=ALU.pow), r=[("nt_ss2", pb_), "cneg"], w=[("nt_rs", pb_)])
            P.add("dve", lambda e: e.scalar_tensor_tensor(out=hb[0:n, :], in0=h_sb[0:n, :], scalar=rs[0:n, 0:1],
                                                           in1=g_bc[0:n, :], op0=ALU.mult, op1=ALU.mult),
                  r=[hkey, ("nt_rs", pb_), "g_bc"], w=[hbk])
            for c in range(8):
                P.add("pe", lambda e, c=c: e.transpose(psT[:, c * 128:c * 128 + n], hb[0:n, c * 128:(c + 1) * 128],
                                                        ident[0:n, 0:n]),
                      r=[hbk, "ident"], w=[KT_])
            P.add("dve", lambda e: e.tensor_copy(out=hnT[:, :, c0:c0 + n],
                                                  in_=psT[:, :].rearrange("p (c t) -> p c t", c=8)[:, :, 0:n]),
                  r=[KT_], w=[("hnT", b)])

        def hnT_keys(c0, n):
            ks = []
            for b in range(NB):
                bc, bn = blk(b)
                if bc < c0 + n and c0 < bc + bn:
                    ks.append(("hnT", b))
            return ks

        ALL_HNT = [("hnT", b) for b in range(NB)]

        def load_g(src_ap):
            P.add("sp", lambda e: [e.dma_start(out=g_bc[:], in_=src_ap.partition_broadcast(128))],
                  w=["g_bc"], dma=("gbc", 1))

        def phase_a():
            with contextlib.ExitStack() as st:
                nb_ = dict(junk=[sb(f"a_junk{i}", [128, D], BF16, st) for i in range(2)],
                           ss=[sb(f"a_ss{i}", [128, 2], F32, st) for i in range(2)],
                           rs=[sb(f"a_rs{i}", [128, 1], F32, st) for i in range(2)],
                           hb=[sb(f"a_hb{i}", [128, D], BF16, st) for i in range(2)])
                hs = [sb(f"a_h{i}", [128, D], F32, st) for i in range(6)]
                load_g(normg_d[0:1, :])
                if units is None:
                    prefetch_w(0, OFF_FOX[0], 256)
                for b in range(NB):
                    c0, n = blk(b)
                    slot = b % 6
                    src = meta_d if b == 0 else x_d[(b - 1) * 128:b * 128, :]
                    P.add("sp", lambda e, slot=slot, n=n, src=src: [e.dma_start(out=hs[slot][0:n, :], in_=src)],
                          w=[("ah", slot)], dma=(f"ah{slot}", 1))
                    norm_tile(b, hs[slot], ("ah", slot), nb_)
                P.barrier()

        wslots = [sb(f"wslot{i}", [128, 8, 512], BF16) for i in range(2)]
        accS = sb("accS", [128, 4, 2, 130], F32)
        wfg = sb("wfg", [128, 8, 4], BF16)
        wfg32 = sb("wfg32", [128, 8, 4], F32)
        wctr = [0]

        wstage = [sb(f"wstage{i}", [128, 1024], F32) for i in range(2)]
        stctr = [0]

        prefetched = {}

        def load_w_steps(l, off, ncols):
            s = wctr[0] % 2
            wctr[0] += 1
            cpc = 1024 // ncols
            pieces = []
            for c0 in range(0, 8, cpc):
                ss = stctr[0] % 2
                stctr[0] += 1
                stv = wstage[ss][:, 0:cpc * ncols].rearrange("p (c n) -> p c n", c=cpc)
                pieces.append((c0, ss, stv))

            def dma(p):
                c0, ss, stv = p
                P.add("sp", lambda e: [e.dma_start(out=stv, in_=win_d[l, :, c0:c0 + cpc, off:off + ncols])],
                      w=[("wst", ss)], dma=(f"wst{ss}", 1))

            def cast(p):
                c0, ss, stv = p
                P.add("act", lambda e: e.activation(out=wslots[s][:, c0:c0 + cpc, 0:ncols], in_=stv, func=AF.Copy),
                      r=[("wst", ss)], w=[("w", s)])

            def stA():
                for p in pieces[0:2]:
                    dma(p)

            def stB():
                for p in pieces[0:2]:
                    cast(p)
                for p in pieces[2:4]:
                    dma(p)

            def stC():
                for p in pieces[2:4]:
                    cast(p)
            return wslots[s], ("w", s), [stA, stB, stC]

        def prefetch_w(l, off, ncols):
            W_, wk_, steps_ = load_w_steps(l, off, ncols)
            for st_ in steps_:
                st_()
            prefetched[(l, off)] = (W_, wk_, [])

        def load_w(l, off, ncols):
            key = (l, off)
            if key in prefetched:
                W, wk, steps = prefetched.pop(key)
            else:
                W, wk, steps = load_w_steps(l, off, ncols)
            for st_ in steps:
                st_()
            return W, wk

        def layer_prep(l):
            lam_init = 0.8 - 0.6 * float(np.exp(-0.3 * l))

            def f(e):
                return [
                    e.dma_start(out=lam_bc[:], in_=lam_d[l:l + 1, :].partition_broadcast(128)),
                    e.dma_start(out=subg[:], in_=subg_d[l:l + 1, :].partition_broadcast(128)),
                    e.dma_start(out=bfcol[0:4, 0:1], in_=bf_d[l]),
                    e.dma_start(out=pscale[:], in_=pscale_d[l]),
                ]
            P.add("sp", f, w=["lprep"], dma=("lprep", 4))
            P.add("pool", lambda e: e.memset(wp_stage[:], 0.0), w=["wp_stage"])

            def f2(e):
                return [
                    e.dma_start(out=wp_stage[0:64, 0, 0:64], in_=wpool_d[l, 0]),
                    e.dma_start(out=wp_stage[64:128, 0, 64:128], in_=wpool_d[l, 1]),
                    e.dma_start(out=wp_stage[0:64, 1, 0:64], in_=wpool_d[l, 2]),
                    e.dma_start(out=wp_stage[64:128, 1, 64:128], in_=wpool_d[l, 3]),
                ]
            P.add("sp", f2, w=["wp_stage"], dma=("wpst", 4))
            P.add("pool", lambda e: e.tensor_copy(out=bd[:], in_=wp_stage[:]), r=["wp_stage"], w=["bd"])
            for i in range(2):
                P.add("dve", lambda e, i=i: e.tensor_tensor(out=junk64[:], in0=lam_bc[:, 128 * i:128 * i + 64],
                                                             in1=lam_bc[:, 128 * i + 64:128 * i + 128], op=ALU.mult),
                      r=["lprep"], w=["junk64"])
                P.add("dve", lambda e, i=i: e.tensor_reduce(out=lamv[:, i:i + 1], in_=junk64[:], axis=AX.X, op=ALU.add),
                      r=["junk64"], w=[("lamv", i)])
            P.add("act", lambda e: e.activation(out=lamv[:, 2:4], in_=lamv[:, 0:2], func=AF.Exp),
                  r=[("lamv", 0), ("lamv", 1)], w=["lamv_e"])
            P.add("dve", lambda e: e.tensor_tensor(out=lamv[:, 4:5], in0=lamv[:, 2:3], in1=lamv[:, 3:4], op=ALU.subtract),
                  r=["lamv_e"], w=["lamv_d"])
            P.add("dve", lambda e: e.tensor_scalar(out=lamv[:, 5:6], in0=lamv[:, 4:5], scalar1=lam_init, scalar2=-1.0,
                                                    op0=ALU.add, op1=ALU.mult), r=["lamv_d"], w=["lamneg"])
            P.add("dve", lambda e: e.tensor_scalar(out=subg2[:], in0=subg[:], scalar1=(1.0 - lam_init) * 0.5, scalar2=None,
                                                    op0=ALU.mult), r=["lprep"], w=["subg2"])
            P.add("dve", lambda e: e.tensor_scalar(out=pscale2[:], in0=pscale[:], scalar1=0.5, scalar2=None, op0=ALU.mult),
                  r=["lprep"], w=["pscale2"])
            P.add("dve", lambda e: e.tensor_scalar(out=bfcol[0:4, 1:2], in0=bfcol[0:4, 0:1], scalar1=-1.0, scalar2=None,
                                                    op0=ALU.mult), r=["lprep"], w=["bfneg"])

        def fox_prep(l):
            biasF, cdiff = lay["biasF"], lay["cdiff"]
            KTl = lay["KT"]
            P.add("pool", lambda e: e.memset(KTl[64:128, :], 0.0), w=["KT", "KTone", "cdiff"])
            with contextlib.ExitStack() as st:
                sp_t = sb("fx_sp", [4, T], F32, st)
                cpos = sb("fx_cpos", [4, T], F32, st)
                ones4 = sb("fx_ones", [4, 512], F32, st)
                cK = sb("fx_cK", [128, 33, 4], F32, st)
                cRb = sb("fx_cRb", [128, 4, 9], F32, st)
                check_sbuf("foxprep")
                P.add("sp", lambda e: [e.dma_start(out=wfg32[:], in_=win_d[l, :, :, OFF_FG:OFF_FG + 4])],
                      w=["wfg32"], dma=("wfg", 1))
                P.add("pool", lambda e: e.tensor_copy(out=wfg[:], in_=wfg32[:]), r=["wfg32"], w=["wfg"])
                P.add("pool", lambda e: e.memset(ones4[:], 1.0), w=["fx_ones"])
                for (c0, n) in COLG:
                    ps, pk = next_sbank()
                    for c in range(8):
                        P.add("pe", lambda e, c=c, ps=ps, c0=c0, n=n: e.matmul(ps[0:4, 0:n], wfg[:, c, 0:4], hnT[:, c, c0:c0 + n],
                                                                                start=(c == 0), stop=(c == 7)),
                              r=["wfg"] + hnT_keys(c0, n), w=[pk])
                    P.add("act", lambda e, ps=ps, c0=c0, n=n: e.activation(out=sp_t[0:4, c0:c0 + n], in_=ps[0:4, 0:n], func=AF.Exp,
                                                                            scale=-1.0, bias=bfcol[0:4, 1:2]),
                          r=[pk, "bfneg"], w=[("fx_sp", c0)])
                    P.add("act", lambda e, c0=c0, n=n: e.activation(out=sp_t[0:4, c0:c0 + n], in_=sp_t[0:4, c0:c0 + n], func=AF.Ln,
                                                                     bias=1.0),
                          r=[("fx_sp", c0)], w=[("fx_sp", c0)])
                for gi_, (c0, n) in enumerate(COLG):
                    init = 0.0 if gi_ == 0 else cpos[0:4, c0 - 1:c0]
                    P.add("dve", lambda e, c0=c0, n=n, init=init: e.tensor_tensor_scan(out=cpos[0:4, c0:c0 + n], data0=ones4[0:4, 0:n],
                                                                                      data1=sp_t[0:4, c0:c0 + n], initial=init,
                                                                                      op0=ALU.mult, op1=ALU.add),
                          r=[("fx_sp", c0), "fx_ones", "fx_cpos"], w=["fx_cpos"])
                psAf = psA[:, 0, :]
                for b in range(NB):
                    c0, n = blk(b)
                    P.add("pe", lambda e, b=b, c0=c0, n=n: e.transpose(psAf[0:n, b * 4:b * 4 + 4], cpos[0:4, c0:c0 + n],
                                                                        identf[0:4, 0:4]),
                          r=["fx_cpos", "identf"], w=[KA[0]])
                P.add("pool", lambda e: e.memset(cK[:].rearrange("p b h -> p (b h)"), 0.0), w=["fx_cK", "fx_cK0"])
                P.add("dve", lambda e: e.tensor_copy(out=cK[0:16, 0, :], in_=psAf[0:16, 0:4]),
                      r=[KA[0]], w=["fx_cK0"])
                P.add("dve", lambda e: e.tensor_copy(out=cK[:, 1:33, :].rearrange("p b h -> p (b h)"), in_=psAf[:, 4:132]),
                      r=[KA[0]], w=["fx_cK"])
                psB = psA[:, 1, :]
                refcols = [0] + [16 + 512 * gi + 256 for gi in range(8)]
                for h in range(4):
                    P.add("pe", lambda e, h=h: e.matmul(psB[:, h * 9:h * 9 + 1], onehot[0:4, h, :], cpos[0:4, 0:1],
                                                         start=True, stop=True),
                          r=["fx_cpos", "consts"], w=[KA[1]])
                    P.add("pe", lambda e, h=h: e.matmul(psB[:, h * 9 + 1:h * 9 + 9], onehot[0:4, h, :],
                                                         cpos[0:4, 272:272 + 512 * 7 + 1:512], start=True, stop=True),
                          r=["fx_cpos", "consts"], w=[KA[1]])
                P.add("dve", lambda e: e.tensor_copy(out=cRb[:].rearrange("p h g -> p (h g)"), in_=psB[:, 0:36]),
                      r=[KA[1]], w=["fx_cRb"])
                for h in range(4):
                    for g in range(9):
                        P.add("dve", lambda e, h=h, g=g: e.tensor_scalar(out=biasF[:, h, g, :], in0=cK[:, :, h],
                                                                          scalar1=cRb[:, h, g:g + 1], scalar2=None,
                                                                          op0=ALU.subtract),
                              r=["fx_cK", "fx_cK0", "fx_cRb"], w=[("biasF", h)])
                for g, (c0, n) in enumerate(COLG):
                    rc = refcols[g]
                    P.add("dve", lambda e, c0=c0, n=n, rc=rc: e.tensor_scalar(out=cdiff[:, c0:c0 + n], in0=cpos[0:4, c0:c0 + n],
                                                                               scalar1=cpos[0:4, rc:rc + 1], scalar2=-1.0,
                                                                               op0=ALU.subtract, op1=ALU.mult),
                          r=["fx_cpos"], w=["cdiff"])
                P.barrier()

        PT = [sb(f"PT{i}", [128, 2, 512], BF16) for i in range(4)]
        kbctr = [0]

        def attention_unit(l, kind, h, mixst, mixst_key, nxt=None, post=None):
            fox = kind == "fox"
            biasF, cdiff = lay["biasF"], lay["cdiff"]
            ns = 1 if fox else 2
            GQ = 512 if fox else GQ_DIFF[h]
            nq = GQ // 128
            vw = 65 if fox else 129
            ve = 64 if fox else 128
            off = OFF_FOX[h] if fox else OFF_DIFF[h]
            ncols = 256 if fox else 512
            with contextlib.ExitStack() as st:
                QT = lay["QT"]
                KT = lay["KT"]
                VA = lay["VA"][:, :, 0:vw + 1]
                G = lay["G"][:, :, 0:ve]
                tz, o1, o2, sm, mtb = lay["tz"], lay["o1"], lay["o2"], lay["sm"], lay["mtb"]
                nS = 4 if fox else 2
                LA = nS - 1

                def S3(sl, nk, c0, c1):
                    if fox:
                        return psS[sl // 2][0:nk, sl % 2:sl % 2 + 1, c0:c1]
                    return psS[sl][0:nk, 0:2, c0:c1]

                def S2(sl, s, nk, c0, c1):
                    if fox:
                        return psS[sl // 2][0:nk, sl % 2, c0:c1]
                    return psS[sl][0:nk, s, c0:c1]

                def Skeys(sl):
                    return [("ps", sl)] if fox else [("ps", 2 * sl), ("ps", 2 * sl + 1)]
                deferred = lay["deferred"]
                check_sbuf("unit")
                W, wk = load_w(l, off, ncols)
                import os
                SK = os.environ.get("K_SKIP", "").split(",")
                if "ones" not in SK:
                    VAfull = lay["VA"]
                    P.add("pool", lambda e: e.memset(VAfull[:, :, :].rearrange("p b w -> p (b w)"), 2.0 if fox else 1.0), w=["VAone", "VA"])
                if fox:
                    P.add("pool", lambda e: e.memset(KT[64:65, :], 1.0), w=["KTone"])
                    P.add("pool", lambda e: e.memset(QT[64:128, :], 0.0), w=["QTaug"])
                    P.add("sp", lambda e: [e.dma_start(out=QT[64:65, :], in_=KT[96 + h:97 + h, :])], r=["cdiff"], w=["QTaug"],
                          dma=("qtaug", 1))
                for (c0, n) in (COLG if "qk" not in SK else []):
                    if fox:
                        ps, pk = next_sbank()
                        for c in range(8):
                            P.add("pe", lambda e, c=c, ps=ps, c0=c0, n=n: e.matmul(ps[:, 0:n], W[:, c, 0:128], hnT[:, c, c0:c0 + n],
                                                                                    start=(c == 0), stop=(c == 7)),
                                  r=[wk] + hnT_keys(c0, n), w=[pk])
                        P.add("act", lambda e, ps=ps, c0=c0, n=n: e.activation(out=QT[0:64, c0:c0 + n], in_=ps[0:64, 0:n], func=AF.Copy,
                                                                                scale=0.125), r=[pk], w=["QT"])
                        P.add("act", lambda e, ps=ps, c0=c0, n=n: e.activation(out=KT[0:64, c0:c0 + n], in_=ps[64:128, 0:n], func=AF.Copy),
                              r=[pk], w=["KT"])
                    else:
                        ps, pk = next_sbank()
                        for c in range(8):
                            P.add("pe", lambda e, c=c, ps=ps, c0=c0, n=n: e.matmul(ps[:, 0:n], W[:, c, 0:128], hnT[:, c, c0:c0 + n],
                                                                                    start=(c == 0), stop=(c == 7)),
                                  r=[wk] + hnT_keys(c0, n), w=[pk])
                        P.add("act", lambda e, ps=ps, c0=c0, n=n: e.activation(out=QT[:, c0:c0 + n], in_=ps[:, 0:n], func=AF.Copy,
                                                                                scale=0.125), r=[pk], w=["QT"])
                        ps2, pk2 = next_sbank()
                        for c in range(8):
                            P.add("pe", lambda e, c=c, ps2=ps2, c0=c0, n=n: e.matmul(ps2[:, 0:n], W[:, c, 128:256], hnT[:, c, c0:c0 + n],
                                                                                      start=(c == 0), stop=(c == 7)),
                                  r=[wk] + hnT_keys(c0, n), w=[pk2])
                        P.add("act", lambda e, ps2=ps2, c0=c0, n=n: e.activation(out=KT[:, c0:c0 + n], in_=ps2[:, 0:n], func=AF.Copy),
                              r=[pk2], w=["KT", "cdiff"])
                vz0 = 128 if fox else 256
                vzw = 2 * ve
                per = 512 // vzw
                batches = [[0]] + [list(range(1 + i, 1 + i + per)) for i in range(0, 32, per)]
                for bl in (batches if "vz" not in SK else []):
                    ps, pk = next_sbank()
                    psv = ps.rearrange("p (j w) -> p j w", w=vzw)
                    n = blk(bl[0])[1]
                    nb = len(bl)
                    for j, b in enumerate(bl):
                        c0 = blk(b)[0]
                        for c in range(8):
                            P.add("pe", lambda e, c=c, j=j, c0=c0, n=n, psv=psv: e.matmul(psv[0:n, j, :], hnT[:, c, c0:c0 + n],
                                                                                            W[:, c, vz0:vz0 + vzw], start=(c == 0), stop=(c == 7)),
                                  r=[wk, ("hnT", b)], w=[pk])
                    b0 = bl[0]
                    VZ = os.environ.get("K_VZ", "").split(",")
                    if "copy" not in VZ:
                        P.add("act", lambda e, psv=psv, n=n, nb=nb, b0=b0: e.activation(out=VA[0:n, b0:b0 + nb, 0:ve], in_=psv[0:n, 0:nb, 0:ve],
                                                                                         func=AF.Copy), r=[pk], w=["VA"])
                    if "tanh" in VZ:
                        continue
                    P.add("act", lambda e, psv=psv, n=n, nb=nb: e.activation(out=tz[0:n, 0:nb, 0:ve], in_=psv[0:n, 0:nb, ve:2 * ve],
                                                                              func=AF.Tanh, scale=0.5), r=[pk], w=["tz"])
                    P.add("dve", lambda e, psv=psv, n=n, nb=nb, b0=b0: e.scalar_tensor_tensor(out=G[0:n, b0:b0 + nb, :], in0=tz[0:n, 0:nb, 0:ve],
                                                                                              scalar=1.0, in1=psv[0:n, 0:nb, ve:2 * ve],
                                                                                              op0=ALU.add, op1=ALU.mult),
                          r=[pk, "tz"], w=["G"])

                RQ = ["QT", "QTaug"] if fox else ["QT"]
                RK = ["KT", "KTone", "cdiff"] if fox else ["KT"]
                krows = (lambda s: slice(0, 128)) if fox else (lambda s: slice(64 * s, 64 * s + 64))
                Mtab = maskc if fox else None

                def acc_region(s, t):
                    if fox:
                        return 0, t * 65
                    if GQ == 256:
                        return s, t * 129
                    if t < 3:
                        return s, t * 129
                    return 2, s * 129

                groups = [("meta", 0, 16, 0)] + [("x", 16 + GQ * gi, GQ, gi) for gi in range(4096 // GQ)]
                import os
                if os.environ.get("K_MAXG"):
                    groups = groups[int(os.environ.get("K_MING", "0")):int(os.environ["K_MAXG"])]
                for f_ in list(deferred):
                    f_()
                deferred.clear()
                pf_steps = []
                if nxt is not None and not os.environ.get("K_MAXG"):
                    Wn_, wkn_, pf_steps = load_w_steps(*nxt)
                    prefetched[(nxt[0], nxt[1])] = (Wn_, wkn_, pf_steps)
                for gidx_, (gk, qc0, qn, gi) in enumerate(groups):
                    if gk == "meta":
                        kblocks = [(0, 0, 16, True)]
                        nqt = 1
                    else:
                        t0 = gi * nq
                        kblocks = [(b, 0, GQ, False) for b in range(0, 1 + t0)]
                        kblocks += [(1 + t0 + kj, kj * 128, GQ - kj * 128, True) for kj in range(nq)]
                        nqt = nq
                    started = set()
                    nkb = len(kblocks)

                    def bias_ap(b, nk):
                        if fox:
                            g9 = 0 if gk == "meta" else 1 + gi
                            return biasF[0:nk, h, g9, b:b + 1]
                        if gk == "meta":
                            return ameta[0:nk, h, 0:1]
                        if b == 0:
                            return ameta[0:nk, h, 1 + gi:2 + gi]
                        d = (b - 1) - gi * nq
                        return atab[0:nk, h, d + 31:d + 32]

                    def emit_qk(i):
                        b, qoff, nqc, diag = kblocks[i]
                        kc0, nk = blk(b)
                        si = (kbctr[0] + i) % nS
                        for s in range(ns):
                            rs_ = krows(s)
                            P.add("pe", lambda e, s=s, rs_=rs_, si=si, nk=nk, kc0=kc0, qoff=qoff, nqc=nqc, qc0=qc0, diag=diag:
                                  e.matmul(S2(si, s, nk, qoff, qoff + nqc), KT[rs_, kc0:kc0 + nk],
                                           QT[rs_, qc0 + qoff:qc0 + qoff + nqc], start=True, stop=not diag,
                                           skip_group_check=diag),
                                  r=RQ + RK, w=[Skeys(si)[s]])
                        if diag and not fox:
                            dn = min(128, nqc)
                            if nk == 16:
                                combos = [(0, ident, mdiff_bf), (1, idsw_bf, mdsw_bf)]
                            else:
                                combos = [(0, ident, mdiff_bf), (1, ident, mdiff_bf), (0, idsw_bf, mdsw_bf), (1, idsw_bf, mdsw_bf)]
                            seen_ = {}
                            for ci, (s, idt, mt_) in enumerate(combos):
                                pr = slice(64 * s, 64 * s + 64)
                                is_last = all(c2[0] != s for c2 in combos[ci + 1:])
                                P.add("pe", lambda e, s=s, pr=pr, idt=idt, mt_=mt_, is_last=is_last, si=si, nk=nk, qoff=qoff, dn=dn:
                                      e.matmul(S2(si, s, nk, qoff, qoff + dn), idt[pr, 0:nk], mt_[pr, h, 0:dn],
                                               start=False, stop=is_last, skip_group_check=True),
                                      r=["ident", "mask_bf", "mask_bf2"], w=[Skeys(si)[s]])
                        if diag and fox:
                            dn = min(128, nqc)
                            if fox:
                                P.add("pe", lambda e, si=si, nk=nk, qoff=qoff, dn=dn:
                                      e.matmul(S2(si, 0, nk, qoff, qoff + dn), ident[0:nk, 0:nk], maskc_bf[0:nk, 0:dn],
                                               start=False, stop=True, skip_group_check=True),
                                      r=["ident", "mask_bf"], w=[Skeys(si)[0]])
                            elif nk == 16:
                                for s in range(ns):
                                    P.add("pe", lambda e, s=s, si=si, nk=nk, qoff=qoff, dn=dn:
                                          e.matmul(S2(si, s, nk, qoff, qoff + dn), ident[0:nk, 0:nk], mdiff_bf[0:nk, h, 0:dn],
                                                   start=False, stop=True, skip_group_check=True),
                                          r=["ident", "mask_bf"], w=[Skeys(si)[s]])
                            else:
                                for (s, hfp) in ((0, 0), (1, 1), (0, 1), (1, 0)):
                                    pr = slice(64 * hfp, 64 * hfp + 64)
                                    P.add("pe", lambda e, s=s, pr=pr, hfp=hfp, si=si, nk=nk, qoff=qoff, dn=dn:
                                          e.matmul(S2(si, s, nk, qoff, qoff + dn), ident[pr, 0:nk], mdiff_bf[pr, h, 0:dn],
                                                   start=False, stop=(hfp == 1 - s), skip_group_check=True),
                                          r=["ident", "mask_bf"], w=[Skeys(si)[s]])

                    def emit_rest(i):
                        b, qoff, nqc, diag = kblocks[i]
                        kc0, nk = blk(b)
                        si = (kbctr[0] + i) % nS
                        pi = (kbctr[0] + i) % 4
                        skeys = Skeys(si)
                        bap = bias_ap(b, nk)
                        P.add("act", lambda e: e.activation(out=PT[pi][0:nk, 0:ns, qoff:qoff + nqc], in_=S3(si, nk, qoff, qoff + nqc),
                                                            func=AF.Exp, bias=bap),
                              r=skeys + ["consts", ("biasF", h)] if fox else skeys + ["consts"], w=[("PT", pi)])
                        tq0 = qoff // 128
                        ntq = max(1, nqc // 128)
                        last = (i == nkb - 1)
                        for tt in range(ntq):
                            t = tq0 + tt
                            qn_t = min(128, nqc)
                            for s in range(ns):
                                bk, co = acc_region(s, t)
                                first = bk not in started
                                started.add(bk)
                                P.add("pe", lambda e, s=s, tt=tt, bk=bk, co=co, first=first, qn_t=qn_t:
                                      e.matmul(psA[0:qn_t, bk, co:co + vw], PT[pi][0:nk, s, qoff + tt * 128:qoff + tt * 128 + qn_t],
                                               VA[0:nk, b, 0:vw], start=first, stop=last, skip_group_check=True),
                                      r=[("PT", pi), "VA", "VAone"], w=[KA[bk]])

                    for i in range(min(LA, nkb)):
                        emit_qk(i)
                    for i in range(nkb):
                        if i + LA < nkb:
                            emit_qk(i + LA)
                        emit_rest(i)
                    kbctr[0] += nkb
                    if pf_steps and gidx_ in (0, 2, 5):
                        pf_steps.pop(0)()
                    for f_ in list(deferred):
                        f_()
                    deferred.clear()

                    for t in range(nqt):
                        n = 16 if gk == "meta" else 128
                        for s_ in range(ns):
                            bk, co = acc_region(s_, t)
                            P.add("dve", lambda e, t=t, s_=s_, bk=bk, co=co, n=n: e.tensor_copy(out=accS[0:n, t, s_, 0:vw], in_=psA[0:n, bk, co:co + vw]),
                                  r=[KA[bk]], w=[("accS", t, s_)])
                    for t in range(nqt):
                        n = 16 if gk == "meta" else 128
                        b = 0 if gk == "meta" else 1 + gi * nq + t
                        tc0 = blk(b)[0]
                        if fox:
                            P.add("dve", lambda e, t=t, n=n: e.reciprocal(out=sm[0:n, 0:1], in_=accS[0:n, t, 0, 64:65]),
                                  r=[("accS", t, 0)], w=["sm"])
                            P.add("dve", lambda e, t=t, n=n, b=b: e.scalar_tensor_tensor(out=mtb[t][0:n, 0:64], in0=accS[0:n, t, 0, 0:64],
                                                                                         scalar=sm[0:n, 0:1], in1=G[0:n, b, :],
                                                                                         op0=ALU.mult, op1=ALU.mult),
                                  r=[("accS", t, 0), "sm", "G"], w=[("mt", t)])
                        else:
                            kA_, kB_ = ("accS", t, 0), ("accS", t, 1)
                            P.add("dve", lambda e, n=n, t=t: e.reciprocal(out=sm[0:n, 0:2],
                                                                          in_=accS[0:n, t, 0:2, 128:129].rearrange("p s o -> p (s o)")),
                                  r=[kA_, kB_], w=["sm0", "sm1"])
                            P.add("dve", lambda e, n=n, t=t: e.tensor_scalar(out=o2[0:n, :], in0=accS[0:n, t, 1, 0:128], scalar1=sm[0:n, 1:2],
                                                                             scalar2=lamv[0:n, 5:6], op0=ALU.mult, op1=ALU.mult),
                                  r=[kB_, "sm1", "lamneg"], w=["o2"])
                            P.add("dve", lambda e, n=n, t=t: e.scalar_tensor_tensor(out=o1[0:n, :], in0=accS[0:n, t, 0, 0:128], scalar=sm[0:n, 0:1],
                                                                                    in1=o2[0:n, :], op0=ALU.mult, op1=ALU.add),
                                  r=[kA_, "sm0", "o2"], w=["o1"])
                            P.add("dve", lambda e, n=n: e.tensor_tensor(out=o2[0:n, :], in0=o1[0:n, :], in1=o1[0:n, :], op=ALU.mult),
                                  r=["o1"], w=["o2"])
                            P.add("dve", lambda e, n=n: e.tensor_reduce(out=sm[0:n, 3:4], in_=o2[0:n, :], axis=AX.X, op=ALU.add),
                                  r=["o2"], w=["sm3"])
                            P.add("dve", lambda e, n=n: e.tensor_scalar(out=sm[0:n, 4:5], in0=sm[0:n, 3:4], scalar1=1.0 / 128, scalar2=1e-5,
                                                                         op0=ALU.mult, op1=ALU.add), r=["sm3"], w=["sm4"])
                            P.add("pool", lambda e, n=n: e.tensor_tensor(out=sm[0:n, 5:6], in0=sm[0:n, 4:5], in1=cneg[0:n, 0:1], op=ALU.pow),
                                  r=["sm4", "cneg"], w=["sm5"])
                            P.add("pool", lambda e, n=n, b=b: e.tensor_tensor(out=o2[0:n, :], in0=G[0:n, b, :], in1=subg2[0:n, :], op=ALU.mult),
                                  r=["G", "subg2", "sm3"], w=["o2"])
                            P.add("dve", lambda e, n=n, t=t: e.scalar_tensor_tensor(out=mtb[t][0:n, :], in0=o1[0:n, :], scalar=sm[0:n, 5:6], in1=o2[0:n, :],
                                                                                     op0=ALU.mult, op1=ALU.mult), r=["o1", "sm5", "o2"], w=[("mt", t)])

                    def flush_T(nqt=nqt, n=(16 if gk == "meta" else 128), tc0=blk(0 if gk == "meta" else 1 + gi * nq)[0]):
                        ew = 64 if fox else 128
                        hh = (h % 2) * 64 if fox else 0
                        for t in range(nqt):
                            P.add("pe", lambda e, t=t: e.transpose(psT[0:ew, t * 128:t * 128 + n], mtb[t][0:n, 0:ew], ident[0:n, 0:n]),
                                  r=[("mt", t), "ident"], w=[KT_])
                        ncol = n if nqt == 1 else nqt * 128
                        P.add("dve", lambda e: e.tensor_copy(out=mixst[hh:hh + ew, tc0:tc0 + ncol], in_=psT[0:ew, 0:ncol]),
                              r=[KT_], w=[mixst_key])
                    deferred.append(flush_T)
                if post is not None:
                    deferred.append(post)

        def pool_unit(l, mixst1, flush):
            with contextlib.ExitStack() as st:
                LV = [sb(f"p_lv{i}", [128, 528], F32, st) for i in range(5)]
                pbf2 = [sb(f"p_pbf{i}", [128, 512], BF16, st) for i in range(2)]
                ptz2 = [sb(f"p_tz{i}", [128, 512], F32, st) for i in range(2)]
                pgz2 = [sb(f"p_gz{i}", [128, 512], F32, st) for i in range(2)]
                ptmp = sb("p_tmp", [128, 16], F32, st)
                W, wk = load_w(l, OFF_POOL, 512)
                def pool_half(hf, mixst, mkey):
                    klev = (1, 2) if hf == 0 else (3, 4)
                    kmax = klev[1]
                    for i in range(5):
                        P.add("pool", lambda e, i=i: e.memset(LV[i][:, 0:16], 0.0), w=[("lv", i)])

                    def stage1(gidx, c0, n):
                        pp = gidx % 2
                        pbf_, ptz_, pgz_ = pbf2[pp], ptz2[pp], pgz2[pp]
                        psu, ku = next_sbank()
                        psz, kz = next_sbank()
                        for c in range(8):
                            P.add("pe", lambda e, c=c: e.matmul(psu[:, 0:n], W[:, c, hf * 128:(hf + 1) * 128], hnT[:, c, c0:c0 + n],
                                                                 start=(c == 0), stop=(c == 7)),
                                  r=[wk] + hnT_keys(c0, n), w=[ku])
                        for c in range(8):
                            P.add("pe", lambda e, c=c: e.matmul(psz[:, 0:n], W[:, c, 256 + hf * 128:256 + (hf + 1) * 128],
                                                                 hnT[:, c, c0:c0 + n], start=(c == 0), stop=(c == 7)),
                                  r=[wk] + hnT_keys(c0, n), w=[kz])
                        P.add("act", lambda e: e.activation(out=LV[0][:, 16:16 + n], in_=psu[:, 0:n], func=AF.Copy), r=[ku], w=[("lv", 0)])
                        P.add("act", lambda e: e.activation(out=ptz_[:, 0:n], in_=psz[:, 0:n], func=AF.Tanh, scale=0.5),
                              r=[kz], w=[("ptz", pp)])
                        P.add("dve", lambda e: e.scalar_tensor_tensor(out=pgz_[:, 0:n], in0=ptz_[:, 0:n], scalar=1.0, in1=psz[:, 0:n],
                                                                      op0=ALU.add, op1=ALU.mult), r=[kz, ("ptz", pp)], w=[("pgz", pp)])
                        for i in range(1, kmax + 1):
                            sh = 1 << (i - 1)
                            P.add("pool", lambda e, i=i, sh=sh: e.tensor_tensor(out=LV[i][:, 16:16 + n], in0=LV[i - 1][:, 16:16 + n],
                                                                                in1=LV[i - 1][:, 16 - sh:16 - sh + n], op=ALU.add),
                                  r=[("lv", i - 1)], w=[("lv", i)])
                        for pr in range(2):
                            k = klev[pr]
                            rows = slice(64 * pr, 64 * pr + 64)
                            if c0 == 0:
                                P.add("dve", lambda e, k=k, rows=rows: e.tensor_tensor(out=ptmp[rows, 0:16], in0=LV[k][rows, 16:32],
                                                                                       in1=invc[rows, hf, :], op=ALU.mult),
                                      r=[("lv", k), "consts"], w=["ptmp"])
                                P.add("dve", lambda e, rows=rows: e.tensor_tensor(out=pbf_[rows, 0:16], in0=ptmp[rows, 0:16], in1=LV[0][rows, 16:32],
                                                                                  op=ALU.subtract), r=["ptmp", ("lv", 0)], w=[("pbf", pp)])
                            else:
                                P.add("dve", lambda e, k=k, rows=rows: e.scalar_tensor_tensor(out=pbf_[rows, 0:n], in0=LV[k][rows, 16:16 + n],
                                                                                              scalar=1.0 / (1 << k), in1=LV[0][rows, 16:16 + n],
                                                                                              op0=ALU.mult, op1=ALU.subtract),
                                      r=[("lv", k), ("lv", 0)], w=[("pbf", pp)])
                        for i in range(0, kmax + 1):
                            P.add("pool", lambda e, i=i: e.tensor_copy(out=LV[i][:, 0:16], in_=LV[i][:, n:n + 16]),
                                  r=[("lv", i)], w=[("lv", i)])

                    def stage2(gidx, c0, n):
                        pp = gidx % 2
                        pbf_, pgz_ = pbf2[pp], pgz2[pp]
                        psy, ky = next_sbank()
                        P.add("pe", lambda e: e.matmul(psy[:, 0:n], bd[:, hf, :], pbf_[:, 0:n], start=True, stop=True),
                              r=["bd", ("pbf", pp)], w=[ky])
                        P.add("dve", lambda e: e.scalar_tensor_tensor(out=mixst[:, c0:c0 + n], in0=psy[:, 0:n],
                                                                      scalar=pscale2[:, hf:hf + 1], in1=pgz_[:, 0:n],
                                                                      op0=ALU.mult, op1=ALU.mult),
                              r=[ky, ("pgz", pp), "pscale2"], w=[mkey])

                    stage1(0, *COLG[0])
                    for gidx, (c0, n) in enumerate(COLG):
                        if gidx + 1 < len(COLG):
                            stage1(gidx + 1, *COLG[gidx + 1])
                        stage2(gidx, c0, n)

                check_sbuf("pool")
                for hf_ in range(2):
                    pool_half(hf_, mixst1, ("mixst", 0))
                    flush(6 + hf_, mixst1, ("mixst", 0))
                P.barrier()

        def phase_c(l, last):
            with contextlib.ExitStack() as st:
                nb_ = dict(junk=[sb(f"c_junk{i}", [128, D], BF16, st) for i in range(2)],
                           ss=[sb(f"c_ss{i}", [128, 2], F32, st) for i in range(2)],
                           rs=[sb(f"c_rs{i}", [128, 1], F32, st) for i in range(2)],
                           hb=[sb(f"c_hb{i}", [128, D], BF16, st) for i in range(2)])
                wo = sb("c_wo", [128, 8, D], BF16, st)
                mixin = [sb(f"c_mix{i}", [128, 8, 512], BF16, st) for i in range(2)]
                hold = [sb(f"c_hold{i}", [128, D], F32, st) for i in range(2)]
                hnew = [sb(f"c_hnew{i}", [128, D], F32, st) for i in range(2)]
                yo = hold
                check_sbuf("phasec")
                for c in range(8):
                    ss = stctr[0] % 2
                    stctr[0] += 1
                    P.add("sp", lambda e, c=c, ss=ss: [e.dma_start(out=wstage[ss][:, :], in_=wout_d[l, :, c, :])],
                          w=[("wst", ss)], dma=(f"wst{ss}", 1))
                    P.add("act", lambda e, c=c, ss=ss: e.activation(out=wo[:, c, :], in_=wstage[ss][:, :], func=AF.Copy),
                          r=[("wst", ss)], w=[("wo", c)])
                load_g(fing_d if last else normg_d[l + 1:l + 2, :])
                if not last and units is None:
                    prefetch_w(l + 1, OFF_FOX[0], 256)
                def part1(gidx, gc0, gn, b, first):
                    ms = gidx % 2
                    if first:
                        P.add("sp", lambda e: [e.dma_start(out=mixin[ms][:, :, 0:gn],
                                                           in_=mixT_d[:, :, gc0:gc0 + gn].rearrange("c p t -> p c t"))],
                              r=[("mixT_d", c) for c in range(8)], w=[("mixin", ms)], dma=(f"mixin{ms}", 1))
                    c0, n = blk(b)
                    lo = c0 - gc0
                    hs = b % 2
                    if l == 0:
                        src = meta_d if b == 0 else x_d[(b - 1) * 128:b * 128, :]
                        rk = []
                    else:
                        src = h_d[c0:c0 + n, :]
                        rk = [("h_d", b)]
                    P.add("sp", lambda e: [e.dma_start(out=hold[hs][0:n, :], in_=src)],
                          r=rk, w=[("hold", hs)], dma=(f"hold{hs}", 1))
                    si = b % 2
                    for half in range(2):
                        for c in range(8):
                            P.add("pe", lambda e, c=c, half=half:
                                  e.matmul(psS[si][0:n, half, :], mixin[ms][:, c, lo:lo + n], wo[:, c, half * 512:(half + 1) * 512],
                                           start=(c == 0), stop=(c == 7)),
                                  r=[("mixin", ms), ("wo", c)], w=[("ps", 2 * si + half)])

                def part2(b):
                    c0, n = blk(b)
                    hs = b % 2
                    si = b % 2
                    P.add("dve", lambda e: e.tensor_tensor(out=hnew[hs][0:n, :].rearrange("p (a f) -> p a f", a=2),
                                                           in0=psS[si][0:n, :, :],
                                                           in1=hold[hs][0:n, :].rearrange("p (a f) -> p a f", a=2), op=ALU.add),
                          r=[("ps", 2 * si), ("ps", 2 * si + 1), ("hold", hs)], w=[("hnew", hs)])
                    if not last:
                        P.add("sp", lambda e: [e.dma_start(out=h_d[c0:c0 + n, :], in_=hnew[hs][0:n, :])],
                              r=[("hnew", hs)], w=[("h_d", b)], dma=(f"hout{hs}", 1))
                        norm_tile(b, hnew[hs], ("hnew", hs), nb_)
                        return
                    if dbg:
                        P.add("sp", lambda e: [e.dma_start(out=h_d[c0:c0 + n, :], in_=hnew[hs][0:n, :])],
                              r=[("hnew", hs)], w=[("h_d", b)], dma=(f"hout{hs}", 1))
                    if b == 0:
                        return
                    pb_ = b % 2
                    jk, ss, rs = nb_["junk"][pb_], nb_["ss"][pb_], nb_["rs"][pb_]
                    P.add("act", lambda e: e.activation(out=jk[:, :], in_=hnew[hs][:, :], func=AF.Square, accum_out=ss[:, 0:1]),
                          r=[("hnew", hs)], w=[("nt_junk", pb_), ("nt_ss", pb_)])
                    P.add("dve", lambda e: e.tensor_scalar(out=ss[:, 1:2], in0=ss[:, 0:1], scalar1=1.0 / D, scalar2=1e-6,
                                                            op0=ALU.mult, op1=ALU.add), r=[("nt_ss", pb_)], w=[("nt_ss2", pb_)])
                    P.add("pool", lambda e: e.tensor_tensor(out=rs[:, 0:1], in0=ss[:, 1:2], in1=cneg[:, 0:1], op=ALU.pow),
                          r=[("nt_ss2", pb_), "cneg"], w=[("nt_rs", pb_)])
                    P.add("dve", lambda e: e.scalar_tensor_tensor(out=yo[hs][:, :], in0=hnew[hs][:, :], scalar=rs[:, 0:1],
                                                                  in1=g_bc[:, :], op0=ALU.mult, op1=ALU.mult),
                          r=[("hnew", hs), ("nt_rs", pb_), "g_bc"], w=[("hold", hs)])
                    P.add("sp", lambda e: [e.dma_start(out=y_d[(b - 1) * 128:b * 128, :], in_=yo[hs][:, :])],
                          r=[("hold", hs)], w=[("y", b)], dma=(f"yout{hs}", 1))

                tiles = []
                for gidx, (gc0, gn) in enumerate(COLG):
                    blocks = [0] if gidx == 0 else list(range(1 + 4 * (gidx - 1), 5 + 4 * (gidx - 1)))
                    for j, b in enumerate(blocks):
                        tiles.append((gidx, gc0, gn, b, j == 0))
                part1(*tiles[0])
                for i in range(len(tiles)):
                    if i + 1 < len(tiles):
                        part1(*tiles[i + 1])
                    part2(tiles[i][3])
                P.barrier()

        phase_a()
        for l in range(nlayers):
            P.layer = l
            last = (l == nlayers - 1)
            layer_prep(l)
            ulist = units if units is not None else ["fox", "diff", "pool"]
            with contextlib.ExitStack() as ust:
                mixst1 = sb("mixst", [128, T], BF16, ust)
                mixst = [mixst1, mixst1]
                lay["biasF"] = sb("biasF", [128, 4, 9, 33], F32, ust)
                lay["QT"] = sb("u_QT", [128, T], BF16, ust)
                lay["KT"] = sb("u_KT", [128, T], BF16, ust)
                lay["cdiff"] = lay["KT"][96:100, :]
                mctr = 0

                def flush(chunk, buf, key):
                    P.add("sp", lambda e: [e.dma_start(out=mixT_d[chunk], in_=buf[:, :])], r=[key], w=[("mixT_d", chunk)],
                          dma=("mixout", 1))

                if "fox" in ulist:
                    fox_prep(l)
                with contextlib.ExitStack() as ust2:
                    lay["VA"] = sb("u_VA", [128, NB, 130], BF16, ust2)
                    lay["G"] = sb("u_G", [128, NB, 128], BF16, ust2)
                    lay["tz"] = sb("u_tz", [128, 4, 128], F32, ust2)
                    lay["o1"] = sb("u_o1", [128, 128], F32, ust2)
                    lay["o2"] = sb("u_o2", [128, 128], F32, ust2)
                    lay["sm"] = sb("u_sm", [128, 8], F32, ust2)
                    lay["mtb"] = [sb(f"u_mt{i}", [128, 128], BF16, ust2) for i in range(4)]
                    lay["deferred"] = []
                    if "fox" in ulist:
                        for h in range(4):
                            i = (h // 2) % 2
                            nxt = (l, OFF_FOX[h + 1], 256) if h < 3 else ((l, OFF_DIFF[0], 512) if "diff" in ulist else None)
                            post = (lambda h=h, i=i: flush(h // 2, mixst[i], ("mixst", 0))) if h % 2 == 1 else None
                            attention_unit(l, "fox", h, mixst[i], ("mixst", 0), nxt, post)
                    if "diff" in ulist:
                        for h in range(4):
                            i = h % 2
                            nxt = (l, OFF_DIFF[h + 1], 512) if h < 3 else ((l, OFF_POOL, 512) if "pool" in ulist else None)
                            post = (lambda h=h, i=i: flush(2 + h, mixst[i], ("mixst", 0)))
                            attention_unit(l, "diff", h, mixst[i], ("mixst", 0), nxt, post)
                    for f_ in list(lay["deferred"]):
                        f_()
                    lay["deferred"].clear()
                    P.barrier()
                if "pool" in ulist:
                    pool_unit(l, mixst1, flush)
                P.barrier()
            phase_c(l, last)
        allkeys = [k for k in list(P.res.keys())]
        P.add("sp", lambda e: e.nop() if hasattr(e, "nop") else None, r=[], w=allkeys)

        P.finalize()
        build_program.peak_sbuf = peak[0]
        with nc.Block() as block:
            @block.tensor
            def _(e):
                P.emit("pe", e)

            @block.scalar
            def _(e):
                P.emit("act", e)

            @block.vector
            def _(e):
                P.emit("dve", e)

            @block.gpsimd
            def _(e):
                P.emit("pool", e)

            @block.sync
            def _(e):
                P.emit("sp", e)
    return nc


def make_consts():
    kk = np.arange(128)[:, None].astype(np.float64)
    qq = np.arange(128)[None, :].astype(np.float64)
    maskc = np.where(qq >= kk, 0.0, NEGM).astype(np.float32)
    mdiff = np.zeros((128, 4, 128), np.float32)
    atab = np.zeros((128, 4, 35), np.float32)
    ameta = np.zeros((128, 4, 17), np.float32)
    for h in range(4):
        s = SLOPES[h]
        m = np.where(qq >= kk, 0.0, -2.0 * s * (kk - qq))
        m = m + np.where((kk >= 64) & (qq < 64), NEGM, 0.0)
        mdiff[:, h, :] = m
        GQ = GQ_DIFF[h]
        for di in range(35):
            d = di - 31
            atab[:, h, di] = s * (128.0 * d + kk[:, 0] - GQ / 2)
        ameta[:, h, 0] = s * kk[:, 0]
        for gi in range(4096 // GQ):
            ameta[:, h, 1 + gi] = s * (kk[:, 0] - (16 + GQ * gi + GQ / 2))
    onehot = np.zeros((4, 4, 128), np.float32)
    for h in range(4):
        onehot[h, h, :] = 1.0
    invc = np.zeros((128, 2, 16), np.float32)
    for hf in range(2):
        for pr in range(2):
            w = 2 ** ((1, 2)[pr] if hf == 0 else (3, 4)[pr])
            t = np.arange(16)
            invc[64 * pr:64 * pr + 64, hf, :] = (1.0 / np.minimum(t + 1, w))[None, :]
    idsw = np.roll(np.eye(128, dtype=np.float32), 64, axis=1)
    mdsw = np.ascontiguousarray(np.roll(mdiff, -64, axis=0))
    return dict(c_maskc=maskc, c_mdiff=mdiff, c_atab=atab, c_ameta=ameta, c_onehot=onehot, c_invc=invc,
                c_idsw=idsw, c_mdiffsw=mdsw)


def relayout_w_in(w_in):
    cols = []
    for h in range(4):
        cols += list(range(h * 64, h * 64 + 64))
        cols += list(range(256 + h * 64, 256 + h * 64 + 64))
        cols += list(range(512 + h * 64, 512 + h * 64 + 64))
        cols += list(range(768 + h * 64, 768 + h * 64 + 64))
    for h in range(4):
        cols += list(range(1028 + h * 128, 1028 + h * 128 + 128))
        cols += list(range(1540 + h * 128, 1540 + h * 128 + 128))
        cols += list(range(2052 + h * 128, 2052 + h * 128 + 128))
        cols += list(range(2564 + h * 128, 2564 + h * 128 + 128))
    cols += list(range(3076, 3076 + 256))
    cols += list(range(3332, 3332 + 256))
    cols += list(range(1024, 1028))
    w = w_in[:, :, np.asarray(cols)]
    w = w.reshape(w.shape[0], 8, 128, 3588).transpose(0, 2, 1, 3)
    return np.ascontiguousarray(w)


_CACHE = {}


def prep_inputs(x, meta_tokens, norm_g, w_in, b_f, lam_q1, lam_k1, lam_q2, lam_k2, subln_g, w_pool, pool_scale,
                w_out, final_g):
    f = lambda a: np.ascontiguousarray(np.asarray(a, dtype=np.float32))
    shared = dict(
        meta=f(meta_tokens), norm_g=f(norm_g), w_in=relayout_w_in(f(w_in)),
        b_f=f(b_f).reshape(L, 4, 1),
        lam=np.ascontiguousarray(np.concatenate([f(lam_q1), f(lam_k1), f(lam_q2), f(lam_k2)], axis=1)),
        subln_g=f(subln_g), w_pool=f(w_pool),
        pool_scale=np.ascontiguousarray(f(pool_scale).reshape(L, 2, 128).transpose(0, 2, 1)),
        w_out=np.ascontiguousarray(f(w_out).reshape(L, 8, 128, D).transpose(0, 2, 1, 3)),
        final_g=f(final_g).reshape(1, D),
    )
    shared.update(make_consts())
    xs = f(x)
    return [dict(shared, x=np.ascontiguousarray(xs[i])) for i in range(xs.shape[0])]


def kernel(**inputs):
    in_maps = prep_inputs(**inputs)
    if "nc" not in _CACHE:
        _CACHE["nc"] = build_program()
    nc = _CACHE["nc"]
    res = run_bass_kernel_spmd(nc, in_maps, core_ids=list(range(8)))
    return np.stack([np.asarray(r["y"]).reshape(S_LEN, D) for r in res.results], axis=0).astype(np.float32)
```
